# Optimizing a Trainium2 kernel written in Bass

```python
import jax, jax.numpy as jnp
from jax import lax
import numpy as np

D_MODEL = 1024
BATCH = 8
SEQ = 8192
DEPTH = 1
DEC_BATCH = 16
DEC_SEQ = 32
PAST_LEN = 2048

CHUNK = 64
SGU_CHUNK = 128
N_SGU_HEADS = 4
SGU_WIDTH = D_MODEL // 2
SGU_HEAD_DIM = SGU_WIDTH // N_SGU_HEADS
N_CONV_GROUPS = 4
CONV_WIDTH = D_MODEL // 2
CONV_K = 3
MIX_WIDTH = SGU_WIDTH + CONV_WIDTH
IN_PROJ = 2 * SGU_WIDTH + 3 * CONV_WIDTH
N_MEM = 256
N_MEM_HEADS = 4
MEM_HEAD_DIM = D_MODEL // N_MEM_HEADS
D_FF = -(-8 * D_MODEL // (3 * 256)) * 256
ALPHA = (2.0 * DEPTH) ** 0.25
BETA = (8.0 * DEPTH) ** -0.25
LN_EPS = 1e-5

kernel_name = 'hybrid_sgu_shortconv_memxattn_step'


def _layernorm(x, g, b):
    xf = x.astype(jnp.float32)
    mu = jnp.mean(xf, axis=-1, keepdims=True)
    xc = xf - mu
    var = jnp.mean(jnp.square(xc), axis=-1, keepdims=True)
    return (xc * lax.rsqrt(var + LN_EPS) * g + b).astype(x.dtype)


def _chunk_mask(n):
    idx = jnp.arange(n) // CHUNK
    return idx[None, :] <= idx[:, None]


def _sgu_prompt(u, v, w_s, b_s):
    bsz, s = v.shape[0], v.shape[1]
    n = s // SGU_CHUNK
    w = jnp.where(_chunk_mask(SGU_CHUNK)[None], w_s, 0.0).astype(v.dtype)
    vc = v.reshape(bsz, n, SGU_CHUNK, N_SGU_HEADS, SGU_HEAD_DIM)
    mixed = jnp.einsum('hij,bnjhc->bnihc', w, vc) + b_s.T[None, None, :, :, None]
    return u * mixed.reshape(u.shape)


def _sgu_sample(u, v, w_s, b_s):
    t = v.shape[1]
    w = jnp.where(_chunk_mask(t)[None], w_s[:, :t, :t], 0.0).astype(v.dtype)
    mixed = jnp.einsum('hij,bjhc->bihc', w, v) + b_s[:, :t].T[None, :, :, None]
    return u * mixed


def _layer(x, conv_prev, mem_k, mem_v, sgu_fn,
           w_in, sgu_ln_g, sgu_ln_b, w_s, b_s, w_conv, w_out, ln_mix_g, ln_mix_b,
           w_q, w_mem_o, ln_mem_g, ln_mem_b,
           w_gate, w_up, w_down, ln_ffn_g, ln_ffn_b):
    bsz, s, _ = x.shape
    z = x @ w_in
    u, v, gb, gc, xin = jnp.split(
        z, np.cumsum([SGU_WIDTH, SGU_WIDTH, CONV_WIDTH, CONV_WIDTH]).tolist(), axis=-1)
    u = u.reshape(bsz, s, N_SGU_HEADS, SGU_HEAD_DIM)
    v = _layernorm(v.reshape(bsz, s, N_SGU_HEADS, SGU_HEAD_DIM), sgu_ln_g, sgu_ln_b)
    a_out = sgu_fn(u, v, w_s, b_s).reshape(bsz, s, SGU_WIDTH)
    cx = gc * xin
    cat = jnp.concatenate([conv_prev, cx], axis=1)
    conv = (cat[:, 0:s] * w_conv[0] + cat[:, 1:s + 1] * w_conv[1]
            + cat[:, 2:s + 2] * w_conv[2])
    conv_state = cat[:, -(CONV_K - 1):]
    b_out = gb * conv
    mix = jnp.concatenate([a_out, b_out], axis=-1) @ w_out
    x = _layernorm(ALPHA * x + mix, ln_mix_g, ln_mix_b)
    q = (x @ w_q).reshape(bsz, s, N_MEM_HEADS, MEM_HEAD_DIM)
    sc = jnp.einsum('bshd,bmhd->bhsm', q, mem_k).astype(jnp.float32) * (MEM_HEAD_DIM ** -0.5)
    p = jax.nn.softmax(sc, axis=-1).astype(x.dtype)
    o = jnp.einsum('bhsm,bmhd->bshd', p, mem_v).reshape(bsz, s, D_MODEL)
    x = _layernorm(ALPHA * x + o @ w_mem_o, ln_mem_g, ln_mem_b)
    f = (jax.nn.silu(x @ w_gate) * (x @ w_up)) @ w_down
    x = _layernorm(ALPHA * x + f, ln_ffn_g, ln_ffn_b)
    return x, conv_state, v


def setup_inputs(seed: int = 0) -> dict:
    key = jax.random.key(seed)
    ks = iter(jax.random.split(key, 40))
    nrm = lambda shape, scale: jax.random.normal(next(ks), shape, jnp.float32) * scale
    gain = lambda shape: 1.0 + nrm(shape, 0.01)
    L = DEPTH
    return {
        'x_prompt': nrm((BATCH, SEQ, D_MODEL), 1.0),
        'x_sample': nrm((DEC_BATCH, DEC_SEQ, D_MODEL), 1.0),
        'cache_mem_k': nrm((L, DEC_BATCH, N_MEM, N_MEM_HEADS, MEM_HEAD_DIM), 1.0),
        'cache_mem_v': nrm((L, DEC_BATCH, N_MEM, N_MEM_HEADS, MEM_HEAD_DIM), BETA),
        'state_conv': nrm((L, DEC_BATCH, CONV_K - 1, CONV_WIDTH), 1.0),
        'mem_prompt': nrm((BATCH, N_MEM, D_MODEL), 1.0),
        'w_in': nrm((L, D_MODEL, IN_PROJ), D_MODEL ** -0.5),
        'sgu_ln_g': gain((L, N_SGU_HEADS, SGU_HEAD_DIM)),
        'sgu_ln_b': nrm((L, N_SGU_HEADS, SGU_HEAD_DIM), 0.01),
        'w_s': nrm((L, N_SGU_HEADS, SGU_CHUNK, SGU_CHUNK), SGU_CHUNK ** -0.5),
        'b_s': gain((L, N_SGU_HEADS, SGU_CHUNK)),
        'w_conv': nrm((L, CONV_K, CONV_WIDTH), CONV_K ** -0.5),
        'w_out': nrm((L, MIX_WIDTH, D_MODEL), BETA * MIX_WIDTH ** -0.5),
        'ln_mix_g': gain((L, D_MODEL)),
        'ln_mix_b': nrm((L, D_MODEL), 0.01),
        'w_q': nrm((L, D_MODEL, D_MODEL), D_MODEL ** -0.5),
        'w_mem_k': nrm((L, D_MODEL, D_MODEL), D_MODEL ** -0.5),
        'w_mem_v': nrm((L, D_MODEL, D_MODEL), BETA * D_MODEL ** -0.5),
        'w_mem_o': nrm((L, D_MODEL, D_MODEL), BETA * D_MODEL ** -0.5),
        'ln_mem_g': gain((L, D_MODEL)),
        'ln_mem_b': nrm((L, D_MODEL), 0.01),
        'w_gate': nrm((L, D_MODEL, D_FF), D_MODEL ** -0.5),
        'w_up': nrm((L, D_MODEL, D_FF), D_MODEL ** -0.5),
        'w_down': nrm((L, D_FF, D_MODEL), BETA * D_FF ** -0.5),
        'ln_ffn_g': gain((L, D_MODEL)),
        'ln_ffn_b': nrm((L, D_MODEL), 0.01),
    }


def reference(x_prompt, x_sample, cache_mem_k, cache_mem_v, state_conv, mem_prompt,
              w_in, sgu_ln_g, sgu_ln_b, w_s, b_s, w_conv, w_out, ln_mix_g, ln_mix_b,
              w_q, w_mem_k, w_mem_v, w_mem_o, ln_mem_g, ln_mem_b,
              w_gate, w_up, w_down, ln_ffn_g, ln_ffn_b):
    hp, hs = x_prompt, x_sample
    bp = x_prompt.shape[0]
    mk_p, mv_p, cv_p, cv_s, sv_s = [], [], [], [], []
    for l in range(DEPTH):
        shared = (w_in[l], sgu_ln_g[l], sgu_ln_b[l], w_s[l], b_s[l], w_conv[l], w_out[l],
                  ln_mix_g[l], ln_mix_b[l], w_q[l], w_mem_o[l], ln_mem_g[l], ln_mem_b[l],
                  w_gate[l], w_up[l], w_down[l], ln_ffn_g[l], ln_ffn_b[l])
        mk = (mem_prompt @ w_mem_k[l]).reshape(bp, N_MEM, N_MEM_HEADS, MEM_HEAD_DIM)
        mv = (mem_prompt @ w_mem_v[l]).reshape(bp, N_MEM, N_MEM_HEADS, MEM_HEAD_DIM)
        pad = jnp.zeros((bp, CONV_K - 1, CONV_WIDTH), hp.dtype)
        hp, cp, _ = _layer(hp, pad, mk, mv, _sgu_prompt, *shared)
        hs, cs, vs = _layer(hs, state_conv[l], cache_mem_k[l], cache_mem_v[l], _sgu_sample, *shared)
        mk_p.append(mk); mv_p.append(mv); cv_p.append(cp); cv_s.append(cs); sv_s.append(vs)
    mem_k_prompt = jnp.stack(mk_p)
    mem_v_prompt = jnp.stack(mv_p)
    conv_prompt = jnp.stack(cv_p)
    conv_sample = jnp.stack(cv_s)
    sgu_v_sample = jnp.stack(sv_s)
    return (hp, hs, mem_k_prompt, mem_v_prompt, conv_prompt, conv_sample, sgu_v_sample)
```

```python
import contextlib
import numpy as np
import concourse.bass as bass
import concourse.mybir as mybir
from concourse.bass_utils import run_bass_kernel_spmd

F32 = mybir.dt.float32
BF16 = mybir.dt.bfloat16
AF = mybir.ActivationFunctionType
ALU = mybir.AluOpType

D = 1024
DFF = 2816
NKFF = DFF // 128
ALPHA = 2.0 ** 0.25
EPS = 1e-5
NCORES = 8

COMPUTE = ("tensor", "vector", "scalar", "gpsimd")
QUEUES = ("tensor", "vector", "scalar", "gpsimd", "sync")


class Buf:
    __slots__ = ("name", "writers", "readers", "excl")

    def __init__(self, name, excl=False):
        self.name = name
        self.excl = excl
        self.writers = []
        self.readers = []


class Op:
    __slots__ = ("eng", "fn", "deps", "idx", "signal", "sig", "dma", "stream", "cnt")

    def __init__(self, eng, fn, dma, stream):
        self.eng = eng
        self.fn = fn
        self.deps = []
        self.idx = -1
        self.signal = False
        self.sig = 0
        self.dma = dma
        self.stream = stream
        self.cnt = 0


class Sched:
    def __init__(self, nc):
        self.nc = nc
        self.q = {e: [] for e in QUEUES}
        self.stream_cnt = {}

    def op(self, eng, fn, reads=(), writes=(), dma=None, extra_deps=()):
        o = Op(eng, fn, dma is not None, dma)
        if dma is not None:
            c = self.stream_cnt.get(dma, 0) + 16
            self.stream_cnt[dma] = c
            o.cnt = c
        deps = []
        for b in reads:
            deps.extend(b.writers)
            if b.excl:
                deps.extend(r for r in b.readers if r.eng != eng)
        for b in writes:
            deps.extend(b.writers)
            deps.extend(b.readers)
        deps.extend(extra_deps)
        seen = set()
        for d in deps:
            if d is o or id(d) in seen:
                continue
            seen.add(id(d))
            o.deps.append(d)
        for b in reads:
            b.readers.append(o)
        for b in writes:
            if b.readers:
                b.writers = [o]
                b.readers = []
            else:
                b.writers.append(o)
        o.idx = len(self.q[eng])
        self.q[eng].append(o)
        return o

    def emit(self, final_wait_eng="sync"):
        nc = self.nc
        for e in QUEUES:
            for o in self.q[e]:
                best = {}
                dmadeps = {}
                for d in o.deps:
                    if d.dma:
                        if d.cnt > dmadeps.get(d.stream, 0):
                            dmadeps[d.stream] = d.cnt
                    else:
                        if d.eng == "tensor" and o.eng == "tensor":
                            continue
                        if d.eng not in best or d.idx > best[d.eng].idx:
                            best[d.eng] = d
                o.deps = (best, dmadeps)
                for d in best.values():
                    d.signal = True
        for e in COMPUTE:
            c = 0
            for o in self.q[e]:
                if o.dma:
                    continue
                if o.signal:
                    c += 1
                    o.sig = c
        with contextlib.ExitStack() as st:
            sems = {e: st.enter_context(nc.semaphore("s_" + e)) for e in COMPUTE}
            dsems = {s: st.enter_context(nc.semaphore("d_" + s)) for s in self.stream_cnt}
            block = st.enter_context(nc.Block())
            for e in QUEUES:
                ops = self.q[e]
                if not ops and e != final_wait_eng:
                    continue

                def body(eng, ops=ops, e=e):
                    waited = {}
                    dwaited = {}
                    for o in ops:
                        best, dmadeps = o.deps
                        for pe_, d in best.items():
                            if waited.get(pe_, 0) < d.sig:
                                eng.wait_ge(sems[pe_], d.sig)
                                waited[pe_] = d.sig
                        for s, c in dmadeps.items():
                            if dwaited.get(s, 0) < c:
                                eng.wait_ge(dsems[s], c)
                                dwaited[s] = c
                        inst = o.fn(eng)
                        if o.dma:
                            inst.then_inc(dsems[o.stream], 16)
                        elif o.signal:
                            inst.then_inc(sems[e], 1)
                    if e == final_wait_eng:
                        for s, c in self.stream_cnt.items():
                            if dwaited.get(s, 0) < c:
                                eng.wait_ge(dsems[s], c)

                getattr(block, e)(body)


def build(NT, debug=False):
    nc = bass.Bass("TRN2", target_bir_lowering=False)
    SEQ = NT * 512

    def din(name, shape, dt=F32):
        return nc.dram_tensor(name, shape, dt, kind="ExternalInput").ap()

    def dout(name, shape, dt=F32):
        return nc.dram_tensor(name, shape, dt, kind="ExternalOutput").ap()

    xp = din("x_prompt", [SEQ, D])
    xs = din("x_sample", [64, D])
    ck = din("cache_k", [2, 256, D])
    cv = din("cache_v", [2, 256, D])
    scv = din("state_conv", [4, 512])
    memp = din("mem_prompt", [256, D])
    w_in = din("w_in", [D, 2560])
    sgu_g = din("sgu_ln_g", [1, 512])
    sgu_b = din("sgu_ln_b", [1, 512])
    w_s = din("w_s", [4, 128, 128])
    b_s = din("b_s", [1, 512])
    w_conv = din("w_conv", [3, 512])
    w_out = din("w_out", [D, D])
    w_q = din("w_q", [D, D])
    w_mk = din("w_mem_k", [D, D])
    w_mv = din("w_mem_v", [D, D])
    w_mo = din("w_mem_o", [D, D])
    w_gate = din("w_gate", [D, DFF])
    w_up = din("w_up", [D, DFF])
    w_down = din("w_down", [DFF, D])
    ln_g = [din("ln_mix_g", [1, D]), din("ln_mem_g", [1, D]), din("ln_ffn_g", [1, D])]
    ln_b = [din("ln_mix_b", [1, D]), din("ln_mem_b", [1, D]), din("ln_ffn_b", [1, D])]

    yp = dout("y_prompt", [SEQ, D])
    ys = dout("y_sample", [64, D])
    mk_o = dout("mem_k_prompt", [256, D])
    mv_o = dout("mem_v_prompt", [256, D])
    cvp_o = dout("conv_prompt", [2, 512])
    cvs_o = dout("conv_sample", [4, 512])
    sv_o = dout("sgu_v_sample", [64, 512])

    pieces = {}

    def kview(w, c0, n):
        return w[:, c0:c0 + n].rearrange("(kc p) n -> p kc n", p=128)

    pieces["in_v"] = [(0, 512, kview(w_in, 512, 512))]
    for m in range(4):
        pieces["in_c%d" % m] = [
            (0, 512, kview(w_in, 2048 + m * 128, 128)),
            (128, 512, kview(w_in, 1536 + m * 128, 128)),
            (256, 512, kview(w_in, 1024 + m * 128, 128)),
            (384, 512, kview(w_in, m * 128, 128)),
        ]
    for nm, w in (("out", w_out), ("q", w_q), ("mo", w_mo), ("mk", w_mk), ("mv", w_mv)):
        for h in range(2):
            pieces["%s%d" % (nm, h)] = [(0, 512, kview(w, h * 512, 512))]
    for nm, w in (("g", w_gate), ("u", w_up)):
        for j in range(6):
            n = 512 if j < 5 else 256
            pieces["%s%d" % (nm, j)] = [(0, n, kview(w, j * 512, n))]
    for j in range(6):
        nk = 4 if j < 5 else 2
        pieces["d%d" % j] = [(0, 1024, w_down[j * 512:j * 512 + nk * 128, :].rearrange("(kc p) n -> p kc n", p=128))]
    piece_names = list(pieces.keys())
    pidx = {n: i for i, n in enumerate(piece_names)}
    NP = len(piece_names)
    wsc = nc.dram_tensor("wsc", [NP, 128, 4096], BF16, kind="Internal").ap()

    tile_seq = (["in_v"] + ["in_c%d" % m for m in range(4)] + ["out0", "out1", "q0", "q1", "mo0", "mo1"]
                + [x for j in range(6) for x in ("g%d" % j, "u%d" % j)] + ["d%d" % j for j in range(6)])
    kv_seq = ["mk0", "mk1", "mv0", "mv1"]
    seq = kv_seq + tile_seq * NT + tile_seq

    with contextlib.ExitStack() as st:
        def T(name, shape, dt):
            return st.enter_context(nc.sbuf_tensor(name, shape, dt))

        NRING = 5
        ring = [T("ring%d" % i, [128, 4096], BF16) for i in range(NRING)]
        xbuf = [T("xbuf%d" % i, [128, 4, D], F32) for i in range(2)]
        F0 = T("F0", [128, 8, 512], BF16)
        F1 = T("F1", [128, 8, 512], BF16)
        F2 = T("F2", [128, 8, 512], BF16)
        uT = T("uT", [128, 4, 512], F32)
        v_ln = T("v_ln", [128, 4, 512], BF16)
        xin_sb = [T("xin0", [128, 512], F32)]
        cx = [T("cx%d" % i, [128, 520], F32) for i in range(2)]
        acc = [T("acc%d" % i, [128, 512], F32) for i in range(2)]
        PT = [T("PT%d" % i, [128, 2, 512], BF16) for i in range(2)]
        rden = T("rden", [128, 512], F32)
        sg = [T("sg%d" % i, [128, 512], F32) for i in range(2)]
        hT = T("hT", [128, NKFF, 512], BF16)
        lng = [T("lng%d" % i, [128, D], F32) for i in range(3)]
        lnb = [T("lnb%d" % i, [128, D], F32) for i in range(3)]
        sgg = T("sgg", [128, 512], F32)
        sgb = T("sgb", [128, 512], F32)
        KT = [T("KT%d" % i, [128, 8, 256], BF16) for i in range(3)]
        V = [T("V%d" % i, [128, 2, D], BF16) for i in range(3)]
        ident = T("ident", [128, 128], F32)
        ones_bf = T("ones_bf", [128, 128], BF16)
        wT = T("wT", [128, 4, 128], BF16)
        wT_s = T("wT_s", [64, 4, 64], BF16)
        bsf = sg[0][0:2, :]
        bshi = PT[0][0:2, 0, :]
        bshif = sg[1][0:2, :]
        bs2 = T("bs2", [128, 4, 128], BF16)
        bs2_s = T("bs2_s", [64, 4, 64], BF16)
        onesK2 = T("onesK2", [128, 128], BF16)
        small_sb = xin_sb[0][0:8, :]
        smallT = T("smallT", [128, 4, 8], F32)
        halo_p = T("halo_p", [128, 4, 2], F32)
        halo_s = T("halo_s", [128, 4, 4], F32)
        cvo = rden[0:4, :]
        st6 = [T("st6_%d" % i, [128, 4, 6], F32) for i in range(2)]
        mv4 = [T("mv4_%d" % i, [128, 4, 2], F32) for i in range(2)]
        veps = [T("veps%d" % i, [128, 4], F32) for i in range(2)]
        rstd = [T("rstd%d" % i, [128, 4], F32) for i in range(2)]
        epsb = T("epsb", [128, 1], F32)
        mhalf = T("mhalf", [128, 4], F32)
        banks = [st.enter_context(nc.psum_tensor("bank%d" % i, [128, 512], F32)) for i in range(8)]

        S = Sched(nc)
        B = {}

        def buf(name, excl=False):
            if name not in B:
                B[name] = Buf(name, excl)
            return B[name]

        def both(name):
            return [buf(name + "_0"), buf(name + "_1")]

        b_bank = [buf("bank%d" % i, True) for i in range(8)]
        b_ring = [buf("ring%d" % i) for i in range(NRING)]
        b_xb = [[buf("xb%d_%d" % (i, s)) for s in range(4)] for i in range(2)]
        b_wsc = [buf("wsc%d" % i) for i in range(NP)]

        bank_ctr = [0]

        def next_bank():
            i = bank_ctr[0] % 8
            bank_ctr[0] += 1
            return banks[i], b_bank[i]

        op = S.op

        op("gpsimd", lambda e: e.memset(ident[:], 0.0), writes=[buf("ident")])
        op("gpsimd", lambda e: e.affine_select(out=ident[:], in_=ident[:], compare_op=ALU.not_equal, fill=1.0,
                                                base=0, pattern=[[-1, 128]], channel_multiplier=1),
           reads=[buf("ident")], writes=[buf("ident")])
        op("gpsimd", lambda e: e.memset(ones_bf[:], 1.0), writes=[buf("ones")])
        op("gpsimd", lambda e: e.memset(epsb[:], EPS), writes=[buf("epsb")])
        op("gpsimd", lambda e: e.memset(mhalf[:], -0.5), writes=[buf("mhalf")])
        op("gpsimd", lambda e: e.memset(halo_p[:], 0.0), writes=[buf("halo_p")])
        op("gpsimd", lambda e: e.memset(small_sb, 0.0), writes=both("xin0"))
        op("gpsimd", lambda e: e.memset(acc[1][:], 0.0), writes=both("acc1"))

        op("gpsimd", lambda e: e.dma_start(out=small_sb[0:4, :], in_=scv), writes=both("xin0"), dma="c0")
        op("gpsimd", lambda e: e.dma_start(out=small_sb[4:7, :], in_=w_conv), writes=both("xin0"), dma="c0")
        op("gpsimd", lambda e: e.dma_start(out=acc[0][:].rearrange("p (h j) -> p h j", h=4),
                                           in_=w_s.rearrange("h i j -> i h j")), writes=both("acc0"), dma="c1")
        ws_s = acc[1][0:64, 0:256].rearrange("p (h j) -> p h j", h=4)
        op("gpsimd", lambda e: e.dma_start(out=ws_s[0:32, :, 0:32], in_=w_s[:, 0:32, 0:32].rearrange("h i j -> i h j")),
           writes=both("acc1"), dma="c2")
        op("gpsimd", lambda e: e.dma_start(out=ws_s[32:64, :, 32:64], in_=w_s[:, 0:32, 0:32].rearrange("h i j -> i h j")),
           writes=both("acc1"), dma="c2")
        op("gpsimd", lambda e: e.dma_start(out=bsf, in_=b_s[0:1, :].to_broadcast([2, 512])), writes=both("sg0"), dma="c3")
        op("gpsimd", lambda e: e.dma_start(out=sgg[:], in_=sgu_g[0:1, :].to_broadcast([128, 512])), writes=[buf("sgg")], dma="c4a")
        op("gpsimd", lambda e: e.dma_start(out=sgb[:], in_=sgu_b[0:1, :].to_broadcast([128, 512])), writes=[buf("sgb")], dma="c4b")
        for i in range(3):
            op("gpsimd", lambda e, i=i: e.dma_start(out=lng[i][:], in_=ln_g[i][0:1, :].to_broadcast([128, D])),
               writes=[buf("lng%d" % i)], dma="c5g%d" % i)
            op("gpsimd", lambda e, i=i: e.dma_start(out=lnb[i][:], in_=ln_b[i][0:1, :].to_broadcast([128, D])),
               writes=[buf("lnb%d" % i)], dma="c5b%d" % i)

        bk, bb = next_bank()
        for m in range(4):
            op("tensor", lambda e, m=m, bk=bk: e.transpose(bk[:, m * 8:m * 8 + 8], small_sb[0:8, m * 128:(m + 1) * 128], ident[0:8, 0:8]),
               reads=both("xin0") + [buf("ident")], writes=[bb])
        op("vector", lambda e, bk=bk: e.tensor_copy(out=smallT[:], in_=bk[:, 0:32].rearrange("p (m c) -> p m c", m=4)),
           reads=[bb], writes=[buf("smallT")])
        bk, bb = next_bank()
        for h in range(4):
            op("tensor", lambda e, h=h, bk=bk: e.transpose(bk[:, h * 128:(h + 1) * 128], acc[0][:, h * 128:(h + 1) * 128], ident[:]),
               reads=both("acc0") + [buf("ident")], writes=[bb])
        op("vector", lambda e, bk=bk: e.tensor_copy(out=wT[:], in_=bk[:].rearrange("p (h i) -> p h i", h=4)),
           reads=[bb], writes=[buf("wT")])
        op("vector", lambda e: e.memset(wT[64:128, :, 0:64], 0.0), reads=[buf("wT")], writes=[buf("wT")])
        bk, bb = next_bank()
        for h in range(4):
            op("tensor", lambda e, h=h, bk=bk: e.transpose(bk[0:64, h * 64:(h + 1) * 64], acc[1][0:64, h * 64:(h + 1) * 64], ident[0:64, 0:64]),
               reads=both("acc1") + [buf("ident")], writes=[bb])
        op("vector", lambda e, bk=bk: e.tensor_copy(out=wT_s[:], in_=bk[0:64, 0:256].rearrange("p (h i) -> p h i", h=4)),
           reads=[bb], writes=[buf("wT_s")])
        op("vector", lambda e: e.tensor_copy(out=bshi, in_=bsf), reads=both("sg0"), writes=both("PT0"))
        op("vector", lambda e: e.tensor_copy(out=bshif, in_=bshi), reads=both("PT0"), writes=both("sg1"))
        op("vector", lambda e: e.tensor_tensor(out=bsf, in0=bsf, in1=bshif, op=ALU.subtract),
           reads=both("sg0") + both("sg1"), writes=both("sg0"))
        op("gpsimd", lambda e: e.memset(bs2[:], 0.0), writes=[buf("bs2")])
        op("gpsimd", lambda e: e.memset(bs2_s[:], 0.0), writes=[buf("bs2_s")])
        op("gpsimd", lambda e: e.memset(onesK2[:], 0.0), writes=[buf("onesK2")])
        op("gpsimd", lambda e: e.memset(onesK2[0:2, :], 1.0), writes=[buf("onesK2")])
        op("vector", lambda e: e.tensor_copy(out=bs2[0:2, :, :].rearrange("p h i -> p (h i)"), in_=bsf), reads=both("sg0"), writes=[buf("bs2")])
        op("vector", lambda e: e.tensor_copy(out=bs2[0:1, :, :].rearrange("p h i -> p (h i)"), in_=bshi[0:1, :]),
           reads=both("PT0") + [buf("bs2")], writes=[buf("bs2")])
        for half in range(2):
            op("vector", lambda e, half=half: e.tensor_copy(out=bs2_s[0:2, :, half * 32:(half + 1) * 32], in_=bs2[0:2, :, 0:32]),
               reads=[buf("bs2")], writes=[buf("bs2_s")])

        NCV = 16
        CONV_AHEAD = 6
        cv_ops = []
        conv_order = []
        for n in seq:
            if n not in conv_order:
                conv_order.append(n)
        conv_state = {"done": 0}
        first_use = {n: seq.index(n) for n in conv_order}

        def emit_convert(n):
            pi = pidx[n]
            for (c0, rowlen, src) in pieces[n]:
                ncols = src.shape[2]
                nk = src.shape[1]
                dst = wsc[pi, :, 0:nk * rowlen].rearrange("p (kc n) -> p kc n", n=rowlen)[:, :, c0:c0 + ncols]
                k = len(cv_ops)
                extra = [cv_ops[k - NCV]] if k >= NCV else []
                o = op("gpsimd", lambda e, dst=dst, src=src: e.dma_start(out=dst, in_=src),
                       writes=[b_wsc[pi]], dma="cv%d" % (k % NCV), extra_deps=extra)
                cv_ops.append(o)

        def ensure_converted(seq_upto):
            while conv_state["done"] < len(conv_order) and first_use[conv_order[conv_state["done"]]] <= seq_upto:
                emit_convert(conv_order[conv_state["done"]])
                conv_state["done"] += 1

        ring_state = {"loaded": 0, "used": 0}
        LOOKAHEAD = NRING - 2

        def emit_ring_loads(upto):
            while ring_state["loaded"] <= min(upto, len(seq) - 1):
                i = ring_state["loaded"]
                ensure_converted(i + CONV_AHEAD)
                slot = i % NRING
                pi = pidx[seq[i]]
                nvalid = max(c0_ + src.shape[2] + (src.shape[1] - 1) * rowlen for (c0_, rowlen, src) in pieces[seq[i]])
                op("sync", lambda e, slot=slot, pi=pi, nvalid=nvalid: e.dma_start(out=ring[slot][:, 0:nvalid], in_=wsc[pi, :, 0:nvalid]),
                   reads=[b_wsc[pi]], writes=[b_ring[slot]], dma="r%d" % slot)
                ring_state["loaded"] += 1

        def use_piece(name):
            i = ring_state["used"]
            assert seq[i] == name, (seq[i], name)
            ring_state["used"] += 1
            emit_ring_loads(i + LOOKAHEAD)
            slot = i % NRING
            return ring[slot], b_ring[slot]

        def transposes_to_feat(xb_i, subs, SP, c0, FT, b_FT):
            for j, s in enumerate(subs):
                for g in range(2):
                    bk, bb = next_bank()
                    for c4 in range(4):
                        c = g * 4 + c4
                        op("tensor", lambda e, bk=bk, s=s, c=c, c4=c4: e.transpose(
                            bk[:, c4 * SP:(c4 + 1) * SP], xbuf[xb_i][0:SP, s, c * 128:(c + 1) * 128], ident[0:SP, 0:SP]),
                           reads=[b_xb[xb_i][s], buf("ident")], writes=[bb])
                    op("scalar", lambda e, bk=bk, j=j, g=g: e.activation(
                        out=FT[:, g * 4:(g + 1) * 4, c0 + j * SP:c0 + (j + 1) * SP],
                        in_=bk[:, 0:4 * SP].rearrange("p (c t) -> p c t", c=4), func=AF.Copy),
                       reads=[bb], writes=[b_FT])

        def layer_norm(xb_i, s, SP, li, hx, on_pool=False, j=0, halves=(0, 1)):
            xv = xbuf[xb_i]
            bx = b_xb[xb_i][s]
            S6, M4, VE, RS = st6[hx], mv4[hx], veps[hx], rstd[hx]
            bS6, bM4, bVE, bRS = buf("st6_%d" % hx), buf("mv4_%d" % hx), buf("veps%d" % hx), buf("rstd%d" % hx)
            for hh in halves:
                op("vector", lambda e, hh=hh: e.bn_stats(out=S6[0:SP, 2 * j + hh, :], in_=xv[0:SP, s, hh * 512:(hh + 1) * 512]),
                   reads=[bx], writes=[bS6])
            if halves == (0,):
                return
            op("vector", lambda e: e.bn_aggr(out=M4[0:SP, 0, :], in_=S6[0:SP, 2 * j:2 * j + 2, :].rearrange("p a b -> p (a b)")),
               reads=[bS6], writes=[bM4])
            op("vector", lambda e: e.tensor_scalar(out=VE[0:SP, 0:1], in0=M4[0:SP, 0, 1:2], scalar1=EPS, scalar2=None, op0=ALU.add),
               reads=[bM4], writes=[bVE])
            op("gpsimd", lambda e: e.tensor_tensor(out=RS[0:SP, 0:1], in0=VE[0:SP, 0:1], in1=mhalf[0:SP, 0:1], op=ALU.pow),
               reads=[bVE, buf("mhalf")], writes=[bRS])
            if on_pool:
                op("vector", lambda e: e.tensor_scalar(out=VE[0:SP, 1:2], in0=M4[0:SP, 0, 0:1], scalar1=-1.0,
                                                       scalar2=RS[0:SP, 0:1], op0=ALU.mult, op1=ALU.mult),
                   reads=[bM4, bRS], writes=[bVE])
                op("scalar", lambda e: e.activation(out=xv[0:SP, s, :], in_=xv[0:SP, s, :], func=AF.Identity,
                                                    scale=RS[0:SP, 0:1], bias=VE[0:SP, 1:2]),
                   reads=[bx, bVE, bRS], writes=[bx])
                op("gpsimd", lambda e: e.tensor_tensor(out=xv[0:SP, s, :], in0=xv[0:SP, s, :], in1=lng[li][0:SP, :], op=ALU.mult),
                   reads=[bx, buf("lng%d" % li)], writes=[bx])
                op("gpsimd", lambda e: e.tensor_tensor(out=xv[0:SP, s, :], in0=xv[0:SP, s, :], in1=lnb[li][0:SP, :], op=ALU.add),
                   reads=[bx, buf("lnb%d" % li)], writes=[bx])
                return
            op("vector", lambda e: e.scalar_tensor_tensor(out=xv[0:SP, s, :], in0=xv[0:SP, s, :], scalar=M4[0:SP, 0, 0:1],
                                                          in1=lng[li][0:SP, :], op0=ALU.subtract, op1=ALU.mult),
               reads=[bx, bM4, buf("lng%d" % li)], writes=[bx])
            op("vector", lambda e: e.scalar_tensor_tensor(out=xv[0:SP, s, :], in0=xv[0:SP, s, :], scalar=RS[0:SP, 0:1],
                                                          in1=lnb[li][0:SP, :], op0=ALU.mult, op1=ALU.add),
               reads=[bx, bRS, buf("lnb%d" % li)], writes=[bx])

        def proj_tok_resid_ln(xb_i, subs, SP, c0, FT, b_FT, pnames, li, hx):
            for half in range(2):
                rg, brg = yield pnames[half]
                for j, s in enumerate(subs):
                    bk, bb = next_bank()
                    for kc in range(8):
                        op("tensor", lambda e, bk=bk, j=j, kc=kc, rg=rg: e.matmul(
                            bk[0:SP, :], lhsT=FT[:, kc, c0 + j * SP:c0 + (j + 1) * SP], rhs=rg[:, kc * 512:(kc + 1) * 512],
                            start=(kc == 0), stop=(kc == 7)),
                           reads=[b_FT, brg], writes=[bb])
                    op("vector", lambda e, bk=bk, s=s, half=half: e.scalar_tensor_tensor(
                        out=xbuf[xb_i][0:SP, s, half * 512:(half + 1) * 512],
                        in0=xbuf[xb_i][0:SP, s, half * 512:(half + 1) * 512], scalar=ALPHA, in1=bk[0:SP, :],
                        op0=ALU.mult, op1=ALU.add),
                       reads=[bb, b_xb[xb_i][s]], writes=[b_xb[xb_i][s]])
                    if half == 0:
                        layer_norm(xb_i, s, SP, li, hx, j=j, halves=(0,))
                    else:
                        layer_norm(xb_i, s, SP, li, hx, j=j, halves=(1,))

        def run_tile(xb_i, subs, SP, hx, segs, kv_sets, is_sample, y_dst, last, prefetch=None, flusher=False):
            NS = len(subs)
            Tn = NS * SP
            c0 = hx * 256
            cs = slice(c0, c0 + Tn)
            nseg = len(segs)
            L = segs[0][1] - segs[0][0]
            sfx = "_%d" % hx
            b_F0, b_F1, b_F2 = buf("F0" + sfx), buf("F1" + sfx), buf("F2" + sfx)
            b_uT, b_vln, b_hT = buf("uT" + sfx), buf("v_ln" + sfx), buf("hT" + sfx)
            WT = wT_s if is_sample else wT
            BS2 = bs2_s if is_sample else bs2
            b_WT = buf("wT_s") if is_sample else buf("wT")
            b_BS2 = buf("bs2_s") if is_sample else buf("bs2")
            S6, M4, VE, RS = st6[hx], mv4[hx], veps[hx], rstd[hx]
            bS6, bM4, bVE, bRS = buf("st6" + sfx), buf("mv4" + sfx), buf("veps%d" % hx), buf("rstd%d" % hx)
            def VT(h, n):
                return acc[h // 2][0:n, hx * 256 + (h % 2) * 128:hx * 256 + (h % 2 + 1) * 128]

            def bVT(h):
                return buf("acc%d%s" % (h // 2, sfx))
            h0 = hx * 256

            transposes_to_feat(xb_i, subs, SP, c0, F0, b_F0)

            rg, brg = yield "in_v"
            for j, s in enumerate(subs):
                bk, bb = next_bank()
                for kc in range(8):
                    op("tensor", lambda e, bk=bk, j=j, kc=kc, rg=rg: e.matmul(
                        bk[0:SP, :], lhsT=F0[:, kc, c0 + j * SP:c0 + (j + 1) * SP], rhs=rg[:, kc * 512:(kc + 1) * 512],
                        start=(kc == 0), stop=(kc == 7)), reads=[b_F0, brg], writes=[bb])
                for h in range(4):
                    op("vector", lambda e, bk=bk, h=h: e.bn_stats(out=S6[0:SP, h, :], in_=bk[0:SP, h * 128:(h + 1) * 128]),
                       reads=[bb], writes=[bS6])
                for h in range(4):
                    op("vector", lambda e, h=h: e.bn_aggr(out=M4[0:SP, h, :], in_=S6[0:SP, h, :]),
                       reads=[bS6], writes=[bM4])
                op("vector", lambda e: e.tensor_scalar(out=VE[0:SP, :], in0=M4[0:SP, :, 1], scalar1=EPS, scalar2=None, op0=ALU.add),
                   reads=[bM4], writes=[bVE])
                op("gpsimd", lambda e: e.tensor_tensor(out=RS[0:SP, :], in0=VE[0:SP, :], in1=mhalf[0:SP, :], op=ALU.pow),
                   reads=[bVE, buf("mhalf")], writes=[bRS])
                for h in range(4):
                    hs = slice(h * 128, (h + 1) * 128)
                    op("vector", lambda e, bk=bk, h=h, hs=hs: e.scalar_tensor_tensor(
                        out=VT(h, SP), in0=bk[0:SP, hs], scalar=M4[0:SP, h, 0:1], in1=sgg[0:SP, hs],
                        op0=ALU.subtract, op1=ALU.mult), reads=[bb, bM4, buf("sgg")], writes=[bVT(h)])
                for h in range(4):
                    hs = slice(h * 128, (h + 1) * 128)
                    if is_sample:
                        op("vector", lambda e, h=h, hs=hs: e.scalar_tensor_tensor(
                            out=VT(h, SP), in0=VT(h, SP), scalar=RS[0:SP, h:h + 1], in1=sgb[0:SP, hs],
                            op0=ALU.mult, op1=ALU.add), reads=[bVT(h), bRS, buf("sgb")], writes=[bVT(h)])
                    else:
                        op("vector", lambda e, s=s, h=h, hs=hs: e.scalar_tensor_tensor(
                            out=v_ln[0:SP, s, hs], in0=VT(h, SP), scalar=RS[0:SP, h:h + 1], in1=sgb[0:SP, hs],
                            op0=ALU.mult, op1=ALU.add), reads=[bVT(h), bRS, buf("sgb")], writes=[b_vln])
                if is_sample:
                    for hp in range(2):
                        src = acc[hp][0:64, 0:256]
                        op("sync", lambda e, hp=hp, src=src: e.dma_start(out=sv_o[:, hp * 256:(hp + 1) * 256], in_=src),
                           reads=[bVT(2 * hp)], dma="o_sv%d" % hp)
                        op("scalar", lambda e, s=s, hp=hp, src=src: e.activation(
                            out=v_ln[0:SP, s, hp * 256:(hp + 1) * 256], in_=src, func=AF.Copy),
                           reads=[bVT(2 * hp)], writes=[b_vln])

            for m in range(4):
                rg, brg = yield "in_c%d" % m
                bks = []
                for g in range(4):
                    bk, bb = next_bank()
                    bks.append((bk, bb))
                    for kc in range(8):
                        op("tensor", lambda e, bk=bk, g=g, kc=kc, rg=rg: e.matmul(
                            bk[:, 0:Tn], lhsT=rg[:, kc * 512 + g * 128:kc * 512 + (g + 1) * 128], rhs=F0[:, kc, cs],
                            start=(kc == 0), stop=(kc == 7)), reads=[b_F0, brg], writes=[bb])
                (bxin, bbxin), (bgc, bbgc), (bgb, bbgb), (bu, bbu) = bks
                sl = m % 2
                xs_ = xin_sb[0][:, h0:h0 + Tn]
                cxs_ = cx[sl][:, hx * 260:hx * 260 + nseg * (L + 2)]
                acs_ = acc[sl][:, h0:h0 + Tn]
                bxs, bcx, bac = buf("xin0" + sfx), buf("cx%d%s" % (sl, sfx)), buf("acc%d%s" % (sl, sfx))
                cx3 = cxs_.rearrange("p (g l) -> p g l", g=nseg)
                ac3 = acs_.rearrange("p (g l) -> p g l", g=nseg)
                op("scalar", lambda e, bxin=bxin, xs_=xs_: e.activation(out=xs_, in_=bxin[:, 0:Tn], func=AF.Copy),
                   reads=[bbxin], writes=[bxs])
                op("vector", lambda e, bgc=bgc, xs_=xs_, cx3=cx3: e.tensor_tensor(
                    out=cx3[:, :, 2:2 + L], in0=bgc[:, 0:Tn].rearrange("p (g l) -> p g l", g=nseg),
                    in1=xs_.rearrange("p (g l) -> p g l", g=nseg), op=ALU.mult),
                   reads=[bbgc, bxs], writes=[bcx])
                if is_sample:
                    op("vector", lambda e, cx3=cx3, m=m: e.tensor_copy(out=cx3[:, :, 0:2], in_=smallT[:, m, 0:4].rearrange("p (b r) -> p b r", b=2)),
                       reads=[buf("smallT")], writes=[bcx])
                else:
                    op("vector", lambda e, cx3=cx3, m=m: e.tensor_copy(out=cx3[:, 0, 0:2], in_=halo_p[:, m, :]),
                       reads=[buf("halo_p")], writes=[bcx])
                op("vector", lambda e, cx3=cx3, ac3=ac3, m=m: e.tensor_scalar(
                    out=ac3, in0=cx3[:, :, 0:L], scalar1=smallT[:, m, 4:5], scalar2=None, op0=ALU.mult),
                   reads=[bcx, buf("smallT")], writes=[bac])
                for k in (1, 2):
                    op("vector", lambda e, cx3=cx3, ac3=ac3, m=m, k=k: e.scalar_tensor_tensor(
                        out=ac3, in0=cx3[:, :, k:k + L], scalar=smallT[:, m, 4 + k:5 + k], in1=ac3,
                        op0=ALU.mult, op1=ALU.add), reads=[bcx, bac, buf("smallT")], writes=[bac])
                op("vector", lambda e, bgb=bgb, acs_=acs_, m=m: e.tensor_tensor(
                    out=F1[:, 4 + m, cs], in0=bgb[:, 0:Tn], in1=acs_, op=ALU.mult),
                   reads=[bbgb, bac], writes=[b_F1])
                if is_sample:
                    op("vector", lambda e, cx3=cx3, m=m: e.tensor_copy(
                        out=halo_s[:, m, :].rearrange("p (b r) -> p b r", b=2), in_=cx3[:, :, L:L + 2]),
                       reads=[bcx], writes=[buf("halo_s")])
                else:
                    op("vector", lambda e, cx3=cx3, m=m: e.tensor_copy(out=halo_p[:, m, :], in_=cx3[:, 0, L:L + 2]),
                       reads=[bcx], writes=[buf("halo_p")])
                op("scalar", lambda e, bu=bu, m=m: e.activation(out=uT[:, m, cs], in_=bu[:, 0:Tn], func=AF.Copy),
                   reads=[bbu], writes=[b_uT])

            if is_sample or last:
                nr = 4 if is_sample else 2
                src_t = halo_s if is_sample else halo_p
                b_src = buf("halo_s") if is_sample else buf("halo_p")
                bk, bb = next_bank()
                for m in range(4):
                    op("tensor", lambda e, bk=bk, m=m: e.transpose(bk[0:nr, m * 128:(m + 1) * 128], src_t[:, m, 0:nr], ident[:]),
                       reads=[b_src, buf("ident")], writes=[bb])
                op("vector", lambda e, bk=bk: e.tensor_copy(out=cvo[0:nr, :], in_=bk[0:nr, :]), reads=[bb], writes=both("rden"))
                dst = cvs_o if is_sample else cvp_o
                op("sync", lambda e, dst=dst: e.dma_start(out=dst, in_=cvo[0:nr, :]), reads=both("rden"), dma="o_cv")

            for h in range(4):
                bk, bb = next_bank()
                for j, s in enumerate(subs):
                    op("tensor", lambda e, bk=bk, j=j, s=s, h=h: e.matmul(
                        bk[:, j * SP:(j + 1) * SP], lhsT=v_ln[0:SP, s, h * 128:(h + 1) * 128], rhs=WT[0:SP, h, 0:SP],
                        start=True, stop=False), reads=[b_vln, b_WT], writes=[bb])
                    op("tensor", lambda e, bk=bk, j=j, h=h: e.matmul(
                        bk[:, j * SP:(j + 1) * SP], lhsT=onesK2[0:SP, :], rhs=BS2[0:SP, h, 0:SP],
                        start=False, stop=True), reads=[buf("onesK2"), b_BS2], writes=[bb])
                op("vector", lambda e, bk=bk, h=h: e.tensor_tensor(
                    out=F1[:, h, cs], in0=bk[:, 0:Tn], in1=uT[:, h, cs], op=ALU.mult),
                   reads=[bb, b_uT], writes=[b_F1])

            yield from proj_tok_resid_ln(xb_i, subs, SP, c0, F1, b_F1, ("out0", "out1"), 0, hx)
            yield None
            transposes_to_feat(xb_i, subs, SP, c0, F0, b_F0)
            for half in range(2):
                rg, brg = yield "q%d" % half
                for m in range(4):
                    bk, bb = next_bank()
                    for kc in range(8):
                        op("tensor", lambda e, bk=bk, m=m, kc=kc, rg=rg: e.matmul(
                            bk[:, 0:Tn], lhsT=rg[:, kc * 512 + m * 128:kc * 512 + (m + 1) * 128], rhs=F0[:, kc, cs],
                            start=(kc == 0), stop=(kc == 7)), reads=[b_F0, brg], writes=[bb])
                    op("scalar", lambda e, bk=bk, half=half, m=m: e.activation(
                        out=F2[:, half * 4 + m, cs], in_=bk[:, 0:Tn], func=AF.Copy), reads=[bb], writes=[b_F2])
            pti = 0
            RD = rden[:, h0:h0 + 256]
            bRD = buf("rden" + sfx)
            for gi, (t0, t1) in enumerate(segs):
                Lg = t1 - t0
                ks = kv_sets[gi]
                bKT, bV = buf("KT%d" % ks), buf("V%d" % ks)
                for h in range(4):
                    pt = PT[pti % 2][:, :, h0:h0 + 256]
                    bpt = buf("PT%d%s" % (pti % 2, sfx))
                    pti += 1
                    for mc in range(2):
                        bk, bb = next_bank()
                        for dc in range(2):
                            op("tensor", lambda e, bk=bk, mc=mc, dc=dc, h=h, ks=ks, t0=t0, t1=t1, Lg=Lg: e.matmul(
                                bk[:, 0:Lg], lhsT=KT[ks][:, h * 2 + dc, mc * 128:(mc + 1) * 128], rhs=F2[:, h * 2 + dc, c0 + t0:c0 + t1],
                                start=(dc == 0), stop=(dc == 1)), reads=[bKT, b_F2], writes=[bb])
                        op("scalar", lambda e, bk=bk, mc=mc, pt=pt, Lg=Lg: e.activation(
                            out=pt[:, mc, 0:Lg], in_=bk[:, 0:Lg], func=AF.Exp, scale=1.0 / 16.0), reads=[bb], writes=[bpt])
                    obk = []
                    for dc in range(2):
                        bk, bb = next_bank()
                        obk.append((bk, bb))
                        for mc in range(2):
                            op("tensor", lambda e, bk=bk, mc=mc, dc=dc, h=h, ks=ks, pt=pt, Lg=Lg: e.matmul(
                                bk[:, 0:Lg], lhsT=V[ks][:, mc, h * 256 + dc * 128:h * 256 + (dc + 1) * 128], rhs=pt[:, mc, 0:Lg],
                                start=(mc == 0), stop=(mc == 1)), reads=[bV, bpt], writes=[bb])
                    bkd, bbd = next_bank()
                    for mc in range(2):
                        op("tensor", lambda e, bkd=bkd, mc=mc, pt=pt, Lg=Lg: e.matmul(
                            bkd[:, 0:Lg], lhsT=ones_bf[:, :], rhs=pt[:, mc, 0:Lg], start=(mc == 0), stop=(mc == 1)),
                           reads=[buf("ones"), bpt], writes=[bbd])
                    op("scalar", lambda e, bkd=bkd, Lg=Lg: e.activation(out=RD[:, 0:Lg], in_=bkd[:, 0:Lg], func=AF.Ln),
                       reads=[bbd], writes=[bRD])
                    op("scalar", lambda e, Lg=Lg: e.activation(out=RD[:, 0:Lg], in_=RD[:, 0:Lg], func=AF.Exp, scale=-1.0),
                       reads=[bRD], writes=[bRD])
                    for dc in range(2):
                        bk, bb = obk[dc]
                        op("vector", lambda e, bk=bk, dc=dc, h=h, t0=t0, t1=t1, Lg=Lg: e.tensor_tensor(
                            out=F1[:, h * 2 + dc, c0 + t0:c0 + t1], in0=bk[:, 0:Lg], in1=RD[:, 0:Lg], op=ALU.mult),
                           reads=[bb, bRD], writes=[b_F1])
            yield from proj_tok_resid_ln(xb_i, subs, SP, c0, F1, b_F1, ("mo0", "mo1"), 1, hx)
            yield None
            transposes_to_feat(xb_i, subs, SP, c0, F0, b_F0)
            if flusher:
                start_ln3_steps(prefetch)
            for j6 in range(6):
                rgg, brgg = yield "g%d" % j6
                rgu, brgu = yield "u%d" % j6
                nm = 4 if j6 < 5 else 2
                rl = 512 if j6 < 5 else 256
                for m in range(nm):
                    bkg, bbg = next_bank()
                    for kc in range(8):
                        op("tensor", lambda e, bkg=bkg, m=m, kc=kc, rgg=rgg, rl=rl: e.matmul(
                            bkg[:, 0:Tn], lhsT=rgg[:, kc * rl + m * 128:kc * rl + (m + 1) * 128], rhs=F0[:, kc, cs],
                            start=(kc == 0), stop=(kc == 7)), reads=[b_F0, brgg], writes=[bbg])
                    bku, bbu2 = next_bank()
                    for kc in range(8):
                        op("tensor", lambda e, bku=bku, m=m, kc=kc, rgu=rgu, rl=rl: e.matmul(
                            bku[:, 0:Tn], lhsT=rgu[:, kc * rl + m * 128:kc * rl + (m + 1) * 128], rhs=F0[:, kc, cs],
                            start=(kc == 0), stop=(kc == 7)), reads=[b_F0, brgu], writes=[bbu2])
                    f = j6 * 4 + m
                    sgs = sg[f % 2][:, h0:h0 + Tn]
                    bsg = buf("sg%d%s" % (f % 2, sfx))
                    op("scalar", lambda e, bkg=bkg, sgs=sgs: e.activation(out=sgs, in_=bkg[:, 0:Tn], func=AF.Silu),
                       reads=[bbg], writes=[bsg])
                    op("vector", lambda e, bku=bku, sgs=sgs, f=f: e.tensor_tensor(
                        out=hT[:, f, cs], in0=bku[:, 0:Tn], in1=sgs, op=ALU.mult),
                       reads=[bbu2, bsg], writes=[b_hT])
                    if flusher:
                        ln3_step()
            if flusher:
                while ln3_steps:
                    ln3_step()
            dbk = [[(banks[hx * 4 + j * 2 + half], b_bank[hx * 4 + j * 2 + half]) for half in range(2)] for j in range(NS)]
            for j6 in range(6):
                rg, brg = yield "d%d" % j6
                nk = 4 if j6 < 5 else 2
                for kl in range(nk):
                    k = j6 * 4 + kl
                    for j in range(NS):
                        for half in range(2):
                            bk, bb = dbk[j][half]
                            op("tensor", lambda e, bk=bk, j=j, half=half, k=k, kl=kl, rg=rg: e.matmul(
                                bk[0:SP, :], lhsT=hT[:, k, c0 + j * SP:c0 + (j + 1) * SP],
                                rhs=rg[:, kl * 1024 + half * 512:kl * 1024 + (half + 1) * 512],
                                start=(k == 0), stop=(k == NKFF - 1)), reads=[b_hT, brg], writes=[bb])
            for j, s in enumerate(subs):
                for half in range(2):
                    bk, bb = dbk[j][half]
                    op("vector", lambda e, bk=bk, s=s, half=half: e.scalar_tensor_tensor(
                        out=xbuf[xb_i][0:SP, s, half * 512:(half + 1) * 512],
                        in0=xbuf[xb_i][0:SP, s, half * 512:(half + 1) * 512], scalar=ALPHA, in1=bk[0:SP, :],
                        op0=ALU.mult, op1=ALU.add), reads=[bb, b_xb[xb_i][s]], writes=[b_xb[xb_i][s]])
            for j, s in enumerate(subs):
                pending.append((xb_i, s, SP, y_dst[j * SP:(j + 1) * SP, :]))

        pending = []
        ln3_steps = []
        st63 = T("st63", [128, 4, 12], F32)
        mv3 = T("mv3", [128, 4, 2], F32)
        ve3 = T("ve3", [128, 4, 2], F32)
        rs3 = T("rs3", [128, 4], F32)

        def ln3_stage(stg, q, xb_i, s, SP, dst):
            xv = xbuf[xb_i]
            bx = b_xb[xb_i][s]
            bS, bM, bV, bR = buf("st63_%d" % q), buf("mv3_%d" % q), buf("ve3_%d" % q), buf("rs3_%d" % q)
            if stg == 0:
                for hh in range(2):
                    op("vector", lambda e, hh=hh: e.bn_stats(out=st63[0:SP, q, hh * 6:(hh + 1) * 6], in_=xv[0:SP, s, hh * 512:(hh + 1) * 512]),
                       reads=[bx], writes=[bS])
                op("vector", lambda e: e.bn_aggr(out=mv3[0:SP, q, :], in_=st63[0:SP, q, :]), reads=[bS], writes=[bM])
            elif stg == 1:
                op("scalar", lambda e: e.activation(out=ve3[0:SP, q, 0:1], in_=mv3[0:SP, q, 1:2], func=AF.Ln, bias=epsb[0:SP, 0:1], scale=1.0),
                   reads=[bM, buf("epsb")], writes=[bV])
                op("scalar", lambda e: e.activation(out=rs3[0:SP, q:q + 1], in_=ve3[0:SP, q, 0:1], func=AF.Exp, scale=-0.5),
                   reads=[bV], writes=[bR])
            elif stg == 2:
                op("vector", lambda e: e.tensor_scalar(out=ve3[0:SP, q, 1:2], in0=mv3[0:SP, q, 0:1], scalar1=-1.0,
                                                       scalar2=rs3[0:SP, q:q + 1], op0=ALU.mult, op1=ALU.mult),
                   reads=[bM, bR], writes=[bV])
            elif stg == 3:
                op("scalar", lambda e: e.activation(out=xv[0:SP, s, :], in_=xv[0:SP, s, :], func=AF.Identity,
                                                    scale=rs3[0:SP, q:q + 1], bias=ve3[0:SP, q, 1:2]),
                   reads=[bx, bV, bR], writes=[bx])
            else:
                op("vector", lambda e: e.tensor_tensor(out=xv[0:SP, s, :], in0=xv[0:SP, s, :], in1=lng[2][0:SP, :], op=ALU.mult),
                   reads=[bx, buf("lng2")], writes=[bx])
                op("vector", lambda e: e.tensor_tensor(out=xv[0:SP, s, :], in0=xv[0:SP, s, :], in1=lnb[2][0:SP, :], op=ALU.add),
                   reads=[bx, buf("lnb2")], writes=[bx])
                op("sync", lambda e: e.dma_start(out=dst, in_=xv[0:SP, s, :]), reads=[bx], dma="yst%d" % xb_i)

        def start_ln3_steps(prefetch):
            items = list(pending)
            del pending[:]
            n = len(items)
            for k in range(n + 4 if n else 0):
                def step(k=k):
                    for q, it in enumerate(items):
                        if 0 <= k - q < 5:
                            ln3_stage(k - q, q, *it)
                ln3_steps.append(step)
            if prefetch is not None:
                ln3_steps.append(prefetch)

        def ln3_step():
            if ln3_steps:
                ln3_steps.pop(0)()

        def flush_pending():
            start_ln3_steps(None)
            while ln3_steps:
                ln3_step()

        def co_run(gens):
            granted = {}
            pos = [ring_state["used"]] * len(gens)
            req = [next(g) for g in gens]
            alive = [True] * len(gens)
            while any(alive):
                for gi, g in enumerate(gens):
                    if not alive[gi]:
                        continue
                    try:
                        if req[gi] is None:
                            req[gi] = g.send(None)
                            if req[gi] is None:
                                continue
                        name = req[gi]
                        i = pos[gi]
                        if i not in granted:
                            assert i == ring_state["used"], (i, ring_state)
                            granted[i] = use_piece(name)
                        assert seq[i] == name, (seq[i], name)
                        pos[gi] += 1
                        req[gi] = g.send(granted[i])
                    except StopIteration:
                        alive[gi] = False

        def k_transposes(src_sub0, dstT, b_dst, col0):
            for mc in range(2):
                for g in range(2):
                    bk, bb = next_bank()
                    for c4 in range(4):
                        c = g * 4 + c4
                        op("tensor", lambda e, bk=bk, mc=mc, c=c, c4=c4: e.transpose(
                            bk[:, c4 * 128:(c4 + 1) * 128], xbuf[1][:, src_sub0 + mc, c * 128:(c + 1) * 128], ident[:]),
                           reads=[b_xb[1][src_sub0 + mc], buf("ident")], writes=[bb])
                    op("scalar", lambda e, bk=bk, mc=mc, g=g: e.activation(
                        out=dstT[:, g * 4:(g + 1) * 4, col0 + mc * 128:col0 + (mc + 1) * 128],
                        in_=bk[:].rearrange("p (c t) -> p c t", c=4), func=AF.Copy), reads=[bb], writes=[b_dst])

        def load_x(t):
            xb_i = t % 2
            if t < NT:
                op("sync", lambda e: e.dma_start(out=xbuf[xb_i][:], in_=xp[t * 512:(t + 1) * 512, :].rearrange("(s p) d -> p s d", p=128)),
                   writes=b_xb[xb_i], dma="x%d" % xb_i)
            else:
                op("sync", lambda e: e.dma_start(out=xbuf[xb_i][0:64, 0, :], in_=xs), writes=b_xb[xb_i], dma="x%d" % xb_i)

        load_x(0)
        if NT > 0:
            op("gpsimd", lambda e: e.dma_start(out=xbuf[1][:, 0:2, :], in_=memp.rearrange("(c p) d -> p c d", p=128)),
               writes=[b_xb[1][0], b_xb[1][1]], dma="mld")
        for b in range(2):
            op("gpsimd", lambda e, b=b: e.dma_start(out=xbuf[1][:, 2:4, :], in_=ck[b].rearrange("(c p) d -> p c d", p=128)),
               writes=[b_xb[1][2], b_xb[1][3]], dma="kld")
            k_transposes(2, KT[b], buf("KT%d" % b), 0)
        emit_ring_loads(LOOKAHEAD)
        for b in range(2):
            op("gpsimd", lambda e, b=b: e.dma_start(out=V[b][:], in_=cv[b].rearrange("(c p) d -> p c d", p=128)),
               writes=[buf("V%d" % b)], dma="vld%d" % b)

        if NT > 0:
            b_F2 = buf("F2_0")
            k_transposes(0, F2, b_F2, 0)
            for wi, (pn, dsto) in enumerate((("mk", mk_o), ("mv", mv_o))):
                for half in range(2):
                    rg, brg = use_piece("%s%d" % (pn, half))
                    if wi == 0:
                        for m in range(4):
                            bk, bb = next_bank()
                            for kc in range(8):
                                op("tensor", lambda e, bk=bk, m=m, kc=kc, rg=rg: e.matmul(
                                    bk[:, 0:256], lhsT=rg[:, kc * 512 + m * 128:kc * 512 + (m + 1) * 128], rhs=F2[:, kc, 0:256],
                                    start=(kc == 0), stop=(kc == 7)), reads=[b_F2, brg], writes=[bb])
                            op("scalar", lambda e, bk=bk, half=half, m=m: e.activation(
                                out=KT[2][:, half * 4 + m, :], in_=bk[:, 0:256], func=AF.Copy), reads=[bb], writes=[buf("KT2")])
                    for mc in range(2):
                        bk, bb = next_bank()
                        for kc in range(8):
                            op("tensor", lambda e, bk=bk, mc=mc, kc=kc, rg=rg: e.matmul(
                                bk[:, :], lhsT=F2[:, kc, mc * 128:(mc + 1) * 128], rhs=rg[:, kc * 512:(kc + 1) * 512],
                                start=(kc == 0), stop=(kc == 7)), reads=[b_F2, brg], writes=[bb])
                        sl = mc % 2
                        op("scalar", lambda e, bk=bk, sl=sl: e.activation(out=sg[sl][:], in_=bk[:], func=AF.Copy),
                           reads=[bb], writes=both("sg%d" % sl))
                        op("sync", lambda e, sl=sl, mc=mc, half=half, dsto=dsto: e.dma_start(
                            out=dsto[mc * 128:(mc + 1) * 128, half * 512:(half + 1) * 512], in_=sg[sl][:]),
                           reads=both("sg%d" % sl), dma="o_kv%d" % sl)
                        if wi == 1:
                            op("vector", lambda e, bk=bk, mc=mc, half=half: e.tensor_copy(
                                out=V[2][:, mc, half * 512:(half + 1) * 512], in_=bk[:]), reads=[bb], writes=[buf("V2")])
        else:
            for n in kv_seq:
                use_piece(n)

        for t in range(NT):
            xb_i = t % 2
            pf = (lambda t=t: load_x(t + 1))
            gA = run_tile(xb_i, [0, 1], 128, 0, [(0, 256)], [2], False, yp[t * 512:t * 512 + 256, :], False, None)
            gB = run_tile(xb_i, [2, 3], 128, 1, [(0, 256)], [2], False, yp[t * 512 + 256:(t + 1) * 512, :], t == NT - 1, pf, True)
            co_run([gA, gB])
        if NT == 0:
            pass
        gS = run_tile(NT % 2, [0], 64, 0, [(0, 32), (32, 64)], [0, 1], True, ys, False, None, True)
        co_run([gS])

        flush_pending()
        assert ring_state["used"] == len(seq), (ring_state, len(seq))
        ensure_converted(len(seq))
        S.emit()
    return nc


_CACHE = {}


def _get_nc(NT):
    if NT not in _CACHE:
        _CACHE[NT] = build(NT)
    return _CACHE[NT]


def make_in_maps(inputs, NT):
    f = lambda a: np.ascontiguousarray(np.asarray(a, dtype=np.float32))
    shared = {
        "w_in": f(inputs["w_in"][0]),
        "sgu_ln_g": f(inputs["sgu_ln_g"][0]).reshape(1, 512),
        "sgu_ln_b": f(inputs["sgu_ln_b"][0]).reshape(1, 512),
        "w_s": f(inputs["w_s"][0]),
        "b_s": f(inputs["b_s"][0]).reshape(1, 512),
        "w_conv": f(inputs["w_conv"][0]),
        "w_out": f(inputs["w_out"][0]),
        "w_q": f(inputs["w_q"][0]),
        "w_mem_k": f(inputs["w_mem_k"][0]),
        "w_mem_v": f(inputs["w_mem_v"][0]),
        "w_mem_o": f(inputs["w_mem_o"][0]),
        "w_gate": f(inputs["w_gate"][0]),
        "w_up": f(inputs["w_up"][0]),
        "w_down": f(inputs["w_down"][0]),
    }
    for n in ("ln_mix_g", "ln_mix_b", "ln_mem_g", "ln_mem_b", "ln_ffn_g", "ln_ffn_b"):
        shared[n] = f(inputs[n][0]).reshape(1, D)
    in_maps = []
    for c in range(NCORES):
        m = dict(shared)
        m["x_prompt"] = f(inputs["x_prompt"][c, :NT * 512])
        m["x_sample"] = f(inputs["x_sample"][2 * c:2 * c + 2]).reshape(64, D)
        m["cache_k"] = f(inputs["cache_mem_k"][0, 2 * c:2 * c + 2]).reshape(2, 256, D)
        m["cache_v"] = f(inputs["cache_mem_v"][0, 2 * c:2 * c + 2]).reshape(2, 256, D)
        m["state_conv"] = f(inputs["state_conv"][0, 2 * c:2 * c + 2]).reshape(4, 512)
        m["mem_prompt"] = f(inputs["mem_prompt"][c])
        in_maps.append(m)
    return in_maps


def run(inputs, NT):
    nc = _get_nc(NT)
    in_maps = make_in_maps(inputs, NT)
    res = run_bass_kernel_spmd(nc, in_maps, core_ids=list(range(NCORES)))
    R = res.results
    y_prompt = np.stack([R[c]["y_prompt"] for c in range(NCORES)]).reshape(NCORES, NT * 512, D)
    y_sample = np.concatenate([R[c]["y_sample"].reshape(2, 32, D) for c in range(NCORES)], axis=0)
    mk = np.stack([R[c]["mem_k_prompt"].reshape(256, 4, 256) for c in range(NCORES)])[None]
    mv = np.stack([R[c]["mem_v_prompt"].reshape(256, 4, 256) for c in range(NCORES)])[None]
    cvp = np.stack([R[c]["conv_prompt"].reshape(2, 512) for c in range(NCORES)])[None]
    cvs = np.concatenate([R[c]["conv_sample"].reshape(2, 2, 512) for c in range(NCORES)], axis=0)[None]
    sv = np.concatenate([R[c]["sgu_v_sample"].reshape(2, 32, 4, 128) for c in range(NCORES)], axis=0)[None]
    outs = (y_prompt, y_sample, mk, mv, cvp, cvs, sv)
    return tuple(np.ascontiguousarray(o.astype(np.float32)) for o in outs)


def kernel(**inputs):
    return run(inputs, 16)
```

```python
import contextlib
import numpy as np
import concourse.bass as bass
import concourse.mybir as mybir
from concourse.bass_utils import run_bass_kernel_spmd

F32 = mybir.dt.float32
BF16 = mybir.dt.bfloat16
AF = mybir.ActivationFunctionType
ALU = mybir.AluOpType

D = 1024
DFF = 2816
NKFF = DFF // 128
ALPHA = 2.0 ** 0.25
EPS = 1e-5
NCORES = 8

COMPUTE = ("tensor", "vector", "scalar", "gpsimd")
QUEUES = ("tensor", "vector", "scalar", "gpsimd", "sync")


class Buf:
    __slots__ = ("name", "writers", "readers", "excl")

    def __init__(self, name, excl=False):
        self.name = name
        self.excl = excl
        self.writers = []
        self.readers = []


class Op:
    __slots__ = ("eng", "fn", "deps", "idx", "signal", "sig", "dma", "stream", "cnt")

    def __init__(self, eng, fn, dma, stream):
        self.eng = eng
        self.fn = fn
        self.deps = []
        self.idx = -1
        self.signal = False
        self.sig = 0
        self.dma = dma
        self.stream = stream
        self.cnt = 0


class Sched:
    def __init__(self, nc):
        self.nc = nc
        self.q = {e: [] for e in QUEUES}
        self.stream_cnt = {}

    def op(self, eng, fn, reads=(), writes=(), dma=None, extra_deps=()):
        o = Op(eng, fn, dma is not None, dma)
        if dma is not None:
            c = self.stream_cnt.get(dma, 0) + 16
            self.stream_cnt[dma] = c
            o.cnt = c
        deps = []
        for b in reads:
            deps.extend(b.writers)
            if b.excl:
                deps.extend(r for r in b.readers if r.eng != eng)
        for b in writes:
            deps.extend(b.writers)
            deps.extend(b.readers)
        deps.extend(extra_deps)
        seen = set()
        for d in deps:
            if d is o or id(d) in seen:
                continue
            seen.add(id(d))
            o.deps.append(d)
        for b in reads:
            b.readers.append(o)
        for b in writes:
            if b.readers:
                b.writers = [o]
                b.readers = []
            else:
                b.writers.append(o)
        o.idx = len(self.q[eng])
        self.q[eng].append(o)
        return o

    def emit(self, final_wait_eng="sync"):
        nc = self.nc
        for e in QUEUES:
            for o in self.q[e]:
                best = {}
                dmadeps = {}
                for d in o.deps:
                    if d.dma:
                        if d.cnt > dmadeps.get(d.stream, 0):
                            dmadeps[d.stream] = d.cnt
                    else:
                        if d.eng == "tensor" and o.eng == "tensor":
                            continue
                        if d.eng not in best or d.idx > best[d.eng].idx:
                            best[d.eng] = d
                o.deps = (best, dmadeps)
                for d in best.values():
                    d.signal = True
        for e in COMPUTE:
            c = 0
            for o in self.q[e]:
                if o.dma:
                    continue
                if o.signal:
                    c += 1
                    o.sig = c
        with contextlib.ExitStack() as st:
            sems = {e: st.enter_context(nc.semaphore("s_" + e)) for e in COMPUTE}
            dsems = {s: st.enter_context(nc.semaphore("d_" + s)) for s in self.stream_cnt}
            block = st.enter_context(nc.Block())
            for e in QUEUES:
                ops = self.q[e]
                if not ops and e != final_wait_eng:
                    continue

                def body(eng, ops=ops, e=e):
                    waited = {}
                    dwaited = {}
                    for o in ops:
                        best, dmadeps = o.deps
                        for pe_, d in best.items():
                            if waited.get(pe_, 0) < d.sig:
                                eng.wait_ge(sems[pe_], d.sig)
                                waited[pe_] = d.sig
                        for s, c in dmadeps.items():
                            if dwaited.get(s, 0) < c:
                                eng.wait_ge(dsems[s], c)
                                dwaited[s] = c
                        inst = o.fn(eng)
                        if o.dma:
                            inst.then_inc(dsems[o.stream], 16)
                        elif o.signal:
                            inst.then_inc(sems[e], 1)
                    if e == final_wait_eng:
                        for s, c in self.stream_cnt.items():
                            if dwaited.get(s, 0) < c:
                                eng.wait_ge(dsems[s], c)

                getattr(block, e)(body)


def build(NT, debug=False):
    nc = bass.Bass("TRN2", target_bir_lowering=False)
    SEQ = NT * 512

    def din(name, shape, dt=F32):
        return nc.dram_tensor(name, shape, dt, kind="ExternalInput").ap()

    def dout(name, shape, dt=F32):
        return nc.dram_tensor(name, shape, dt, kind="ExternalOutput").ap()

    xp = din("x_prompt", [SEQ, D])
    xs = din("x_sample", [64, D])
    ck = din("cache_k", [2, 256, D])
    cv = din("cache_v", [2, 256, D])
    scv = din("state_conv", [4, 512])
    memp = din("mem_prompt", [256, D])
    w_in = din("w_in", [D, 2560])
    sgu_g = din("sgu_ln_g", [1, 512])
    sgu_b = din("sgu_ln_b", [1, 512])
    w_s = din("w_s", [4, 128, 128])
    b_s = din("b_s", [1, 512])
    w_conv = din("w_conv", [3, 512])
    w_out = din("w_out", [D, D])
    w_q = din("w_q", [D, D])
    w_mk = din("w_mem_k", [D, D])
    w_mv = din("w_mem_v", [D, D])
    w_mo = din("w_mem_o", [D, D])
    w_gate = din("w_gate", [D, DFF])
    w_up = din("w_up", [D, DFF])
    w_down = din("w_down", [DFF, D])
    ln_g = [din("ln_mix_g", [1, D]), din("ln_mem_g", [1, D]), din("ln_ffn_g", [1, D])]
    ln_b = [din("ln_mix_b", [1, D]), din("ln_mem_b", [1, D]), din("ln_ffn_b", [1, D])]

    yp = dout("y_prompt", [SEQ, D])
    ys = dout("y_sample", [64, D])
    mk_o = dout("mem_k_prompt", [256, D])
    mv_o = dout("mem_v_prompt", [256, D])
    cvp_o = dout("conv_prompt", [2, 512])
    cvs_o = dout("conv_sample", [4, 512])
    sv_o = dout("sgu_v_sample", [64, 512])

    pieces = {}

    def kview(w, c0, n):
        return w[:, c0:c0 + n].rearrange("(kc p) n -> p kc n", p=128)

    pieces["in_v"] = [(0, 512, kview(w_in, 512, 512))]
    for m in range(4):
        pieces["in_c%d" % m] = [
            (0, 512, kview(w_in, 2048 + m * 128, 128)),
            (128, 512, kview(w_in, 1536 + m * 128, 128)),
            (256, 512, kview(w_in, 1024 + m * 128, 128)),
            (384, 512, kview(w_in, m * 128, 128)),
        ]
    for nm, w in (("out", w_out), ("q", w_q), ("mo", w_mo), ("mk", w_mk), ("mv", w_mv)):
        for h in range(2):
            pieces["%s%d" % (nm, h)] = [(0, 512, kview(w, h * 512, 512))]
    for nm, w in (("g", w_gate), ("u", w_up)):
        for j in range(6):
            n = 512 if j < 5 else 256
            pieces["%s%d" % (nm, j)] = [(0, n, kview(w, j * 512, n))]
    for j in range(6):
        nk = 4 if j < 5 else 2
        pieces["d%d" % j] = [(0, 1024, w_down[j * 512:j * 512 + nk * 128, :].rearrange("(kc p) n -> p kc n", p=128))]
    piece_names = list(pieces.keys())
    pidx = {n: i for i, n in enumerate(piece_names)}
    NP = len(piece_names)
    wsc = nc.dram_tensor("wsc", [NP, 128, 4096], BF16, kind="Internal").ap()

    tile_seq = (["in_v"] + ["in_c%d" % m for m in range(4)] + ["out0", "out1", "q0", "q1", "mo0", "mo1"]
                + [x for j in range(6) for x in ("g%d" % j, "u%d" % j)] + ["d%d" % j for j in range(6)])
    kv_seq = ["mk0", "mk1", "mv0", "mv1"]
    seq = kv_seq + tile_seq * NT + tile_seq

    with contextlib.ExitStack() as st:
        def T(name, shape, dt):
            return st.enter_context(nc.sbuf_tensor(name, shape, dt))

        NRING = 5
        ring = [T("ring%d" % i, [128, 4096], BF16) for i in range(NRING)]
        xbuf = [T("xbuf%d" % i, [128, 4, D], F32) for i in range(2)]
        F0 = T("F0", [128, 8, 512], BF16)
        F1 = T("F1", [128, 8, 512], BF16)
        F2 = T("F2", [128, 8, 512], BF16)
        uT = T("uT", [128, 4, 512], F32)
        v_ln = T("v_ln", [128, 4, 512], BF16)
        xin_sb = [T("xin0", [128, 512], F32)]
        cx = [T("cx%d" % i, [128, 520], F32) for i in range(2)]
        acc = [T("acc%d" % i, [128, 512], F32) for i in range(2)]
        PT = [T("PT%d" % i, [128, 2, 512], BF16) for i in range(2)]
        rden = T("rden", [128, 512], F32)
        sg = [T("sg%d" % i, [128, 512], F32) for i in range(2)]
        hT = T("hT", [128, NKFF, 512], BF16)
        lng = [T("lng%d" % i, [128, D], F32) for i in range(3)]
        lnb = [T("lnb%d" % i, [128, D], F32) for i in range(3)]
        sgg = T("sgg", [128, 512], F32)
        sgb = T("sgb", [128, 512], F32)
        KT = [T("KT%d" % i, [128, 8, 256], BF16) for i in range(3)]
        V = [T("V%d" % i, [128, 2, D], BF16) for i in range(3)]
        ident = T("ident", [128, 128], F32)
        ones_bf = T("ones_bf", [128, 128], BF16)
        wT = T("wT", [128, 4, 128], BF16)
        wT_s = T("wT_s", [64, 4, 64], BF16)
        bsf = sg[0][0:2, :]
        bshi = PT[0][0:2, 0, :]
        bshif = sg[1][0:2, :]
        bs2 = T("bs2", [128, 4, 128], BF16)
        bs2_s = T("bs2_s", [64, 4, 64], BF16)
        onesK2 = T("onesK2", [128, 128], BF16)
        small_sb = xin_sb[0][0:8, :]
        smallT = T("smallT", [128, 4, 8], F32)
        halo_p = T("halo_p", [128, 4, 2], F32)
        halo_s = T("halo_s", [128, 4, 4], F32)
        cvo = rden[0:4, :]
        st6 = [T("st6_%d" % i, [128, 4, 6], F32) for i in range(2)]
        mv4 = [T("mv4_%d" % i, [128, 4, 2], F32) for i in range(2)]
        veps = [T("veps%d" % i, [128, 4], F32) for i in range(2)]
        rstd = [T("rstd%d" % i, [128, 4], F32) for i in range(2)]
        epsb = T("epsb", [128, 1], F32)
        mhalf = T("mhalf", [128, 4], F32)
        banks = [st.enter_context(nc.psum_tensor("bank%d" % i, [128, 512], F32)) for i in range(8)]

        S = Sched(nc)
        B = {}

        def buf(name, excl=False):
            if name not in B:
                B[name] = Buf(name, excl)
            return B[name]

        def both(name):
            return [buf(name + "_0"), buf(name + "_1")]

        b_bank = [buf("bank%d" % i, True) for i in range(8)]
        b_ring = [buf("ring%d" % i) for i in range(NRING)]
        b_xb = [[buf("xb%d_%d" % (i, s)) for s in range(4)] for i in range(2)]
        b_wsc = [buf("wsc%d" % i) for i in range(NP)]

        bank_ctr = [0]

        def next_bank():
            i = bank_ctr[0] % 8
            bank_ctr[0] += 1
            return banks[i], b_bank[i]

        op = S.op

        op("gpsimd", lambda e: e.memset(ident[:], 0.0), writes=[buf("ident")])
        op("gpsimd", lambda e: e.affine_select(out=ident[:], in_=ident[:], compare_op=ALU.not_equal, fill=1.0,
                                                base=0, pattern=[[-1, 128]], channel_multiplier=1),
           reads=[buf("ident")], writes=[buf("ident")])
        op("gpsimd", lambda e: e.memset(ones_bf[:], 1.0), writes=[buf("ones")])
        op("gpsimd", lambda e: e.memset(epsb[:], EPS), writes=[buf("epsb")])
        op("gpsimd", lambda e: e.memset(mhalf[:], -0.5), writes=[buf("mhalf")])
        op("gpsimd", lambda e: e.memset(halo_p[:], 0.0), writes=[buf("halo_p")])
        op("gpsimd", lambda e: e.memset(small_sb, 0.0), writes=both("xin0"))
        op("gpsimd", lambda e: e.memset(acc[1][:], 0.0), writes=both("acc1"))

        op("gpsimd", lambda e: e.dma_start(out=small_sb[0:4, :], in_=scv), writes=both("xin0"), dma="c0")
        op("gpsimd", lambda e: e.dma_start(out=small_sb[4:7, :], in_=w_conv), writes=both("xin0"), dma="c0")
        op("gpsimd", lambda e: e.dma_start(out=acc[0][:].rearrange("p (h j) -> p h j", h=4),
                                           in_=w_s.rearrange("h i j -> i h j")), writes=both("acc0"), dma="c1")
        ws_s = acc[1][0:64, 0:256].rearrange("p (h j) -> p h j", h=4)
        op("gpsimd", lambda e: e.dma_start(out=ws_s[0:32, :, 0:32], in_=w_s[:, 0:32, 0:32].rearrange("h i j -> i h j")),
           writes=both("acc1"), dma="c2")
        op("gpsimd", lambda e: e.dma_start(out=ws_s[32:64, :, 32:64], in_=w_s[:, 0:32, 0:32].rearrange("h i j -> i h j")),
           writes=both("acc1"), dma="c2")
        op("gpsimd", lambda e: e.dma_start(out=bsf, in_=b_s[0:1, :].to_broadcast([2, 512])), writes=both("sg0"), dma="c3")
        op("gpsimd", lambda e: e.dma_start(out=sgg[:], in_=sgu_g[0:1, :].to_broadcast([128, 512])), writes=[buf("sgg")], dma="c4a")
        op("gpsimd", lambda e: e.dma_start(out=sgb[:], in_=sgu_b[0:1, :].to_broadcast([128, 512])), writes=[buf("sgb")], dma="c4b")
        for i in range(3):
            op("gpsimd", lambda e, i=i: e.dma_start(out=lng[i][:], in_=ln_g[i][0:1, :].to_broadcast([128, D])),
               writes=[buf("lng%d" % i)], dma="c5g%d" % i)
            op("gpsimd", lambda e, i=i: e.dma_start(out=lnb[i][:], in_=ln_b[i][0:1, :].to_broadcast([128, D])),
               writes=[buf("lnb%d" % i)], dma="c5b%d" % i)

        bk, bb = next_bank()
        for m in range(4):
            op("tensor", lambda e, m=m, bk=bk: e.transpose(bk[:, m * 8:m * 8 + 8], small_sb[0:8, m * 128:(m + 1) * 128], ident[0:8, 0:8]),
               reads=both("xin0") + [buf("ident")], writes=[bb])
        op("vector", lambda e, bk=bk: e.tensor_copy(out=smallT[:], in_=bk[:, 0:32].rearrange("p (m c) -> p m c", m=4)),
           reads=[bb], writes=[buf("smallT")])
        bk, bb = next_bank()
        for h in range(4):
            op("tensor", lambda e, h=h, bk=bk: e.transpose(bk[:, h * 128:(h + 1) * 128], acc[0][:, h * 128:(h + 1) * 128], ident[:]),
               reads=both("acc0") + [buf("ident")], writes=[bb])
        op("vector", lambda e, bk=bk: e.tensor_copy(out=wT[:], in_=bk[:].rearrange("p (h i) -> p h i", h=4)),
           reads=[bb], writes=[buf("wT")])
        op("vector", lambda e: e.memset(wT[64:128, :, 0:64], 0.0), reads=[buf("wT")], writes=[buf("wT")])
        bk, bb = next_bank()
        for h in range(4):
            op("tensor", lambda e, h=h, bk=bk: e.transpose(bk[0:64, h * 64:(h + 1) * 64], acc[1][0:64, h * 64:(h + 1) * 64], ident[0:64, 0:64]),
               reads=both("acc1") + [buf("ident")], writes=[bb])
        op("vector", lambda e, bk=bk: e.tensor_copy(out=wT_s[:], in_=bk[0:64, 0:256].rearrange("p (h i) -> p h i", h=4)),
           reads=[bb], writes=[buf("wT_s")])
        op("vector", lambda e: e.tensor_copy(out=bshi, in_=bsf), reads=both("sg0"), writes=both("PT0"))
        op("vector", lambda e: e.tensor_copy(out=bshif, in_=bshi), reads=both("PT0"), writes=both("sg1"))
        op("vector", lambda e: e.tensor_tensor(out=bsf, in0=bsf, in1=bshif, op=ALU.subtract),
           reads=both("sg0") + both("sg1"), writes=both("sg0"))
        op("gpsimd", lambda e: e.memset(bs2[:], 0.0), writes=[buf("bs2")])
        op("gpsimd", lambda e: e.memset(bs2_s[:], 0.0), writes=[buf("bs2_s")])
        op("gpsimd", lambda e: e.memset(onesK2[:], 0.0), writes=[buf("onesK2")])
        op("gpsimd", lambda e: e.memset(onesK2[0:2, :], 1.0), writes=[buf("onesK2")])
        op("vector", lambda e: e.tensor_copy(out=bs2[0:2, :, :].rearrange("p h i -> p (h i)"), in_=bsf), reads=both("sg0"), writes=[buf("bs2")])
        op("vector", lambda e: e.tensor_copy(out=bs2[0:1, :, :].rearrange("p h i -> p (h i)"), in_=bshi[0:1, :]),
           reads=both("PT0") + [buf("bs2")], writes=[buf("bs2")])
        for half in range(2):
            op("vector", lambda e, half=half: e.tensor_copy(out=bs2_s[0:2, :, half * 32:(half + 1) * 32], in_=bs2[0:2, :, 0:32]),
               reads=[buf("bs2")], writes=[buf("bs2_s")])

        NCV = 16
        CONV_AHEAD = 6
        cv_ops = []
        conv_order = []
        for n in seq:
            if n not in conv_order:
                conv_order.append(n)
        conv_state = {"done": 0}
        first_use = {n: seq.index(n) for n in conv_order}

        def emit_convert(n):
            pi = pidx[n]
            for (c0, rowlen, src) in pieces[n]:
                ncols = src.shape[2]
                nk = src.shape[1]
                dst = wsc[pi, :, 0:nk * rowlen].rearrange("p (kc n) -> p kc n", n=rowlen)[:, :, c0:c0 + ncols]
                k = len(cv_ops)
                extra = [cv_ops[k - NCV]] if k >= NCV else []
                o = op("gpsimd", lambda e, dst=dst, src=src: e.dma_start(out=dst, in_=src),
                       writes=[b_wsc[pi]], dma="cv%d" % (k % NCV), extra_deps=extra)
                cv_ops.append(o)

        def ensure_converted(seq_upto):
            while conv_state["done"] < len(conv_order) and first_use[conv_order[conv_state["done"]]] <= seq_upto:
                emit_convert(conv_order[conv_state["done"]])
                conv_state["done"] += 1

        ring_state = {"loaded": 0, "used": 0}
        LOOKAHEAD = NRING - 2

        def emit_ring_loads(upto):
            while ring_state["loaded"] <= min(upto, len(seq) - 1):
                i = ring_state["loaded"]
                ensure_converted(i + CONV_AHEAD)
                slot = i % NRING
                pi = pidx[seq[i]]
                nvalid = max(c0_ + src.shape[2] + (src.shape[1] - 1) * rowlen for (c0_, rowlen, src) in pieces[seq[i]])
                op("sync", lambda e, slot=slot, pi=pi, nvalid=nvalid: e.dma_start(out=ring[slot][:, 0:nvalid], in_=wsc[pi, :, 0:nvalid]),
                   reads=[b_wsc[pi]], writes=[b_ring[slot]], dma="r%d" % slot)
                ring_state["loaded"] += 1

        def use_piece(name):
            i = ring_state["used"]
            assert seq[i] == name, (seq[i], name)
            ring_state["used"] += 1
            emit_ring_loads(i + LOOKAHEAD)
            slot = i % NRING
            return ring[slot], b_ring[slot]

        def transposes_to_feat(xb_i, subs, SP, c0, FT, b_FT):
            for j, s in enumerate(subs):
                for g in range(2):
                    bk, bb = next_bank()
                    for c4 in range(4):
                        c = g * 4 + c4
                        op("tensor", lambda e, bk=bk, s=s, c=c, c4=c4: e.transpose(
                            bk[:, c4 * SP:(c4 + 1) * SP], xbuf[xb_i][0:SP, s, c * 128:(c + 1) * 128], ident[0:SP, 0:SP]),
                           reads=[b_xb[xb_i][s], buf("ident")], writes=[bb])
                    op("scalar", lambda e, bk=bk, j=j, g=g: e.activation(
                        out=FT[:, g * 4:(g + 1) * 4, c0 + j * SP:c0 + (j + 1) * SP],
                        in_=bk[:, 0:4 * SP].rearrange("p (c t) -> p c t", c=4), func=AF.Copy),
                       reads=[bb], writes=[b_FT])

        def layer_norm(xb_i, s, SP, li, hx, on_pool=False, j=0, halves=(0, 1)):
            xv = xbuf[xb_i]
            bx = b_xb[xb_i][s]
            S6, M4, VE, RS = st6[hx], mv4[hx], veps[hx], rstd[hx]
            bS6, bM4, bVE, bRS = buf("st6_%d" % hx), buf("mv4_%d" % hx), buf("veps%d" % hx), buf("rstd%d" % hx)
            for hh in halves:
                op("vector", lambda e, hh=hh: e.bn_stats(out=S6[0:SP, 2 * j + hh, :], in_=xv[0:SP, s, hh * 512:(hh + 1) * 512]),
                   reads=[bx], writes=[bS6])
            if halves == (0,):
                return
            op("vector", lambda e: e.bn_aggr(out=M4[0:SP, 0, :], in_=S6[0:SP, 2 * j:2 * j + 2, :].rearrange("p a b -> p (a b)")),
               reads=[bS6], writes=[bM4])
            op("vector", lambda e: e.tensor_scalar(out=VE[0:SP, 0:1], in0=M4[0:SP, 0, 1:2], scalar1=EPS, scalar2=None, op0=ALU.add),
               reads=[bM4], writes=[bVE])
            op("gpsimd", lambda e: e.tensor_tensor(out=RS[0:SP, 0:1], in0=VE[0:SP, 0:1], in1=mhalf[0:SP, 0:1], op=ALU.pow),
               reads=[bVE, buf("mhalf")], writes=[bRS])
            if on_pool:
                op("vector", lambda e: e.tensor_scalar(out=VE[0:SP, 1:2], in0=M4[0:SP, 0, 0:1], scalar1=-1.0,
                                                       scalar2=RS[0:SP, 0:1], op0=ALU.mult, op1=ALU.mult),
                   reads=[bM4, bRS], writes=[bVE])
                op("scalar", lambda e: e.activation(out=xv[0:SP, s, :], in_=xv[0:SP, s, :], func=AF.Identity,
                                                    scale=RS[0:SP, 0:1], bias=VE[0:SP, 1:2]),
                   reads=[bx, bVE, bRS], writes=[bx])
                op("gpsimd", lambda e: e.tensor_tensor(out=xv[0:SP, s, :], in0=xv[0:SP, s, :], in1=lng[li][0:SP, :], op=ALU.mult),
                   reads=[bx, buf("lng%d" % li)], writes=[bx])
                op("gpsimd", lambda e: e.tensor_tensor(out=xv[0:SP, s, :], in0=xv[0:SP, s, :], in1=lnb[li][0:SP, :], op=ALU.add),
                   reads=[bx, buf("lnb%d" % li)], writes=[bx])
                return
            op("vector", lambda e: e.scalar_tensor_tensor(out=xv[0:SP, s, :], in0=xv[0:SP, s, :], scalar=M4[0:SP, 0, 0:1],
                                                          in1=lng[li][0:SP, :], op0=ALU.subtract, op1=ALU.mult),
               reads=[bx, bM4, buf("lng%d" % li)], writes=[bx])
            op("vector", lambda e: e.scalar_tensor_tensor(out=xv[0:SP, s, :], in0=xv[0:SP, s, :], scalar=RS[0:SP, 0:1],
                                                          in1=lnb[li][0:SP, :], op0=ALU.mult, op1=ALU.add),
               reads=[bx, bRS, buf("lnb%d" % li)], writes=[bx])

        def proj_tok_resid_ln(xb_i, subs, SP, c0, FT, b_FT, pnames, li, hx):
            for half in range(2):
                rg, brg = yield pnames[half]
                for j, s in enumerate(subs):
                    bk, bb = next_bank()
                    for kc in range(8):
                        op("tensor", lambda e, bk=bk, j=j, kc=kc, rg=rg: e.matmul(
                            bk[0:SP, :], lhsT=FT[:, kc, c0 + j * SP:c0 + (j + 1) * SP], rhs=rg[:, kc * 512:(kc + 1) * 512],
                            start=(kc == 0), stop=(kc == 7)),
                           reads=[b_FT, brg], writes=[bb])
                    op("vector", lambda e, bk=bk, s=s, half=half: e.scalar_tensor_tensor(
                        out=xbuf[xb_i][0:SP, s, half * 512:(half + 1) * 512],
                        in0=xbuf[xb_i][0:SP, s, half * 512:(half + 1) * 512], scalar=ALPHA, in1=bk[0:SP, :],
                        op0=ALU.mult, op1=ALU.add),
                       reads=[bb, b_xb[xb_i][s]], writes=[b_xb[xb_i][s]])
                    if half == 0:
                        layer_norm(xb_i, s, SP, li, hx, j=j, halves=(0,))
                    else:
                        layer_norm(xb_i, s, SP, li, hx, j=j, halves=(1,))

        def run_tile(xb_i, subs, SP, hx, segs, kv_sets, is_sample, y_dst, last, prefetch=None, flusher=False):
            NS = len(subs)
            Tn = NS * SP
            c0 = hx * 256
            cs = slice(c0, c0 + Tn)
            nseg = len(segs)
            L = segs[0][1] - segs[0][0]
            sfx = "_%d" % hx
            b_F0, b_F1, b_F2 = buf("F0" + sfx), buf("F1" + sfx), buf("F2" + sfx)
            b_uT, b_vln, b_hT = buf("uT" + sfx), buf("v_ln" + sfx), buf("hT" + sfx)
            WT = wT_s if is_sample else wT
            BS2 = bs2_s if is_sample else bs2
            b_WT = buf("wT_s") if is_sample else buf("wT")
            b_BS2 = buf("bs2_s") if is_sample else buf("bs2")
            S6, M4, VE, RS = st6[hx], mv4[hx], veps[hx], rstd[hx]
            bS6, bM4, bVE, bRS = buf("st6" + sfx), buf("mv4" + sfx), buf("veps%d" % hx), buf("rstd%d" % hx)
            def VT(h, n):
                return acc[h // 2][0:n, hx * 256 + (h % 2) * 128:hx * 256 + (h % 2 + 1) * 128]

            def bVT(h):
                return buf("acc%d%s" % (h // 2, sfx))
            h0 = hx * 256

            transposes_to_feat(xb_i, subs, SP, c0, F0, b_F0)

            rg, brg = yield "in_v"
            for j, s in enumerate(subs):
                bk, bb = next_bank()
                for kc in range(8):
                    op("tensor", lambda e, bk=bk, j=j, kc=kc, rg=rg: e.matmul(
                        bk[0:SP, :], lhsT=F0[:, kc, c0 + j * SP:c0 + (j + 1) * SP], rhs=rg[:, kc * 512:(kc + 1) * 512],
                        start=(kc == 0), stop=(kc == 7)), reads=[b_F0, brg], writes=[bb])
                for h in range(4):
                    op("vector", lambda e, bk=bk, h=h: e.bn_stats(out=S6[0:SP, h, :], in_=bk[0:SP, h * 128:(h + 1) * 128]),
                       reads=[bb], writes=[bS6])
                for h in range(4):
                    op("vector", lambda e, h=h: e.bn_aggr(out=M4[0:SP, h, :], in_=S6[0:SP, h, :]),
                       reads=[bS6], writes=[bM4])
                op("vector", lambda e: e.tensor_scalar(out=VE[0:SP, :], in0=M4[0:SP, :, 1], scalar1=EPS, scalar2=None, op0=ALU.add),
                   reads=[bM4], writes=[bVE])
                op("gpsimd", lambda e: e.tensor_tensor(out=RS[0:SP, :], in0=VE[0:SP, :], in1=mhalf[0:SP, :], op=ALU.pow),
                   reads=[bVE, buf("mhalf")], writes=[bRS])
                for h in range(4):
                    hs = slice(h * 128, (h + 1) * 128)
                    op("vector", lambda e, bk=bk, h=h, hs=hs: e.scalar_tensor_tensor(
                        out=VT(h, SP), in0=bk[0:SP, hs], scalar=M4[0:SP, h, 0:1], in1=sgg[0:SP, hs],
                        op0=ALU.subtract, op1=ALU.mult), reads=[bb, bM4, buf("sgg")], writes=[bVT(h)])
                for h in range(4):
                    hs = slice(h * 128, (h + 1) * 128)
                    if is_sample:
                        op("vector", lambda e, h=h, hs=hs: e.scalar_tensor_tensor(
                            out=VT(h, SP), in0=VT(h, SP), scalar=RS[0:SP, h:h + 1], in1=sgb[0:SP, hs],
                            op0=ALU.mult, op1=ALU.add), reads=[bVT(h), bRS, buf("sgb")], writes=[bVT(h)])
                    else:
                        op("vector", lambda e, s=s, h=h, hs=hs: e.scalar_tensor_tensor(
                            out=v_ln[0:SP, s, hs], in0=VT(h, SP), scalar=RS[0:SP, h:h + 1], in1=sgb[0:SP, hs],
                            op0=ALU.mult, op1=ALU.add), reads=[bVT(h), bRS, buf("sgb")], writes=[b_vln])
                if is_sample:
                    for hp in range(2):
                        src = acc[hp][0:64, 0:256]
                        op("sync", lambda e, hp=hp, src=src: e.dma_start(out=sv_o[:, hp * 256:(hp + 1) * 256], in_=src),
                           reads=[bVT(2 * hp)], dma="o_sv%d" % hp)
                        op("scalar", lambda e, s=s, hp=hp, src=src: e.activation(
                            out=v_ln[0:SP, s, hp * 256:(hp + 1) * 256], in_=src, func=AF.Copy),
                           reads=[bVT(2 * hp)], writes=[b_vln])

            for m in range(4):
                rg, brg = yield "in_c%d" % m
                bks = []
                for g in range(4):
                    bk, bb = next_bank()
                    bks.append((bk, bb))
                    for kc in range(8):
                        op("tensor", lambda e, bk=bk, g=g, kc=kc, rg=rg: e.matmul(
                            bk[:, 0:Tn], lhsT=rg[:, kc * 512 + g * 128:kc * 512 + (g + 1) * 128], rhs=F0[:, kc, cs],
                            start=(kc == 0), stop=(kc == 7)), reads=[b_F0, brg], writes=[bb])
                (bxin, bbxin), (bgc, bbgc), (bgb, bbgb), (bu, bbu) = bks
                sl = m % 2
                xs_ = xin_sb[0][:, h0:h0 + Tn]
                cxs_ = cx[sl][:, hx * 260:hx * 260 + nseg * (L + 2)]
                acs_ = acc[sl][:, h0:h0 + Tn]
                bxs, bcx, bac = buf("xin0" + sfx), buf("cx%d%s" % (sl, sfx)), buf("acc%d%s" % (sl, sfx))
                cx3 = cxs_.rearrange("p (g l) -> p g l", g=nseg)
                ac3 = acs_.rearrange("p (g l) -> p g l", g=nseg)
                op("scalar", lambda e, bxin=bxin, xs_=xs_: e.activation(out=xs_, in_=bxin[:, 0:Tn], func=AF.Copy),
                   reads=[bbxin], writes=[bxs])
                op("vector", lambda e, bgc=bgc, xs_=xs_, cx3=cx3: e.tensor_tensor(
                    out=cx3[:, :, 2:2 + L], in0=bgc[:, 0:Tn].rearrange("p (g l) -> p g l", g=nseg),
                    in1=xs_.rearrange("p (g l) -> p g l", g=nseg), op=ALU.mult),
                   reads=[bbgc, bxs], writes=[bcx])
                if is_sample:
                    op("vector", lambda e, cx3=cx3, m=m: e.tensor_copy(out=cx3[:, :, 0:2], in_=smallT[:, m, 0:4].rearrange("p (b r) -> p b r", b=2)),
                       reads=[buf("smallT")], writes=[bcx])
                else:
                    op("vector", lambda e, cx3=cx3, m=m: e.tensor_copy(out=cx3[:, 0, 0:2], in_=halo_p[:, m, :]),
                       reads=[buf("halo_p")], writes=[bcx])
                op("vector", lambda e, cx3=cx3, ac3=ac3, m=m: e.tensor_scalar(
                    out=ac3, in0=cx3[:, :, 0:L], scalar1=smallT[:, m, 4:5], scalar2=None, op0=ALU.mult),
                   reads=[bcx, buf("smallT")], writes=[bac])
                for k in (1, 2):
                    op("vector", lambda e, cx3=cx3, ac3=ac3, m=m, k=k: e.scalar_tensor_tensor(
                        out=ac3, in0=cx3[:, :, k:k + L], scalar=smallT[:, m, 4 + k:5 + k], in1=ac3,
                        op0=ALU.mult, op1=ALU.add), reads=[bcx, bac, buf("smallT")], writes=[bac])
                op("vector", lambda e, bgb=bgb, acs_=acs_, m=m: e.tensor_tensor(
                    out=F1[:, 4 + m, cs], in0=bgb[:, 0:Tn], in1=acs_, op=ALU.mult),
                   reads=[bbgb, bac], writes=[b_F1])
                if is_sample:
                    op("vector", lambda e, cx3=cx3, m=m: e.tensor_copy(
                        out=halo_s[:, m, :].rearrange("p (b r) -> p b r", b=2), in_=cx3[:, :, L:L + 2]),
                       reads=[bcx], writes=[buf("halo_s")])
                else:
                    op("vector", lambda e, cx3=cx3, m=m: e.tensor_copy(out=halo_p[:, m, :], in_=cx3[:, 0, L:L + 2]),
                       reads=[bcx], writes=[buf("halo_p")])
                op("scalar", lambda e, bu=bu, m=m: e.activation(out=uT[:, m, cs], in_=bu[:, 0:Tn], func=AF.Copy),
                   reads=[bbu], writes=[b_uT])

            if is_sample or last:
                nr = 4 if is_sample else 2
                src_t = halo_s if is_sample else halo_p
                b_src = buf("halo_s") if is_sample else buf("halo_p")
                bk, bb = next_bank()
                for m in range(4):
                    op("tensor", lambda e, bk=bk, m=m: e.transpose(bk[0:nr, m * 128:(m + 1) * 128], src_t[:, m, 0:nr], ident[:]),
                       reads=[b_src, buf("ident")], writes=[bb])
                op("vector", lambda e, bk=bk: e.tensor_copy(out=cvo[0:nr, :], in_=bk[0:nr, :]), reads=[bb], writes=both("rden"))
                dst = cvs_o if is_sample else cvp_o
                op("sync", lambda e, dst=dst: e.dma_start(out=dst, in_=cvo[0:nr, :]), reads=both("rden"), dma="o_cv")

            for h in range(4):
                bk, bb = next_bank()
                for j, s in enumerate(subs):
                    op("tensor", lambda e, bk=bk, j=j, s=s, h=h: e.matmul(
                        bk[:, j * SP:(j + 1) * SP], lhsT=v_ln[0:SP, s, h * 128:(h + 1) * 128], rhs=WT[0:SP, h, 0:SP],
                        start=True, stop=False), reads=[b_vln, b_WT], writes=[bb])
                    op("tensor", lambda e, bk=bk, j=j, h=h: e.matmul(
                        bk[:, j * SP:(j + 1) * SP], lhsT=onesK2[0:SP, :], rhs=BS2[0:SP, h, 0:SP],
                        start=False, stop=True), reads=[buf("onesK2"), b_BS2], writes=[bb])
                op("vector", lambda e, bk=bk, h=h: e.tensor_tensor(
                    out=F1[:, h, cs], in0=bk[:, 0:Tn], in1=uT[:, h, cs], op=ALU.mult),
                   reads=[bb, b_uT], writes=[b_F1])

            yield from proj_tok_resid_ln(xb_i, subs, SP, c0, F1, b_F1, ("out0", "out1"), 0, hx)
            yield None
            transposes_to_feat(xb_i, subs, SP, c0, F0, b_F0)
            for half in range(2):
                rg, brg = yield "q%d" % half
                for m in range(4):
                    bk, bb = next_bank()
                    for kc in range(8):
                        op("tensor", lambda e, bk=bk, m=m, kc=kc, rg=rg: e.matmul(
                            bk[:, 0:Tn], lhsT=rg[:, kc * 512 + m * 128:kc * 512 + (m + 1) * 128], rhs=F0[:, kc, cs],
                            start=(kc == 0), stop=(kc == 7)), reads=[b_F0, brg], writes=[bb])
                    op("scalar", lambda e, bk=bk, half=half, m=m: e.activation(
                        out=F2[:, half * 4 + m, cs], in_=bk[:, 0:Tn], func=AF.Copy), reads=[bb], writes=[b_F2])
            pti = 0
            RD = rden[:, h0:h0 + 256]
            bRD = buf("rden" + sfx)
            for gi, (t0, t1) in enumerate(segs):
                Lg = t1 - t0
                ks = kv_sets[gi]
                bKT, bV = buf("KT%d" % ks), buf("V%d" % ks)
                for h in range(4):
                    pt = PT[pti % 2][:, :, h0:h0 + 256]
                    bpt = buf("PT%d%s" % (pti % 2, sfx))
                    pti += 1
                    for mc in range(2):
                        bk, bb = next_bank()
                        for dc in range(2):
                            op("tensor", lambda e, bk=bk, mc=mc, dc=dc, h=h, ks=ks, t0=t0, t1=t1, Lg=Lg: e.matmul(
                                bk[:, 0:Lg], lhsT=KT[ks][:, h * 2 + dc, mc * 128:(mc + 1) * 128], rhs=F2[:, h * 2 + dc, c0 + t0:c0 + t1],
                                start=(dc == 0), stop=(dc == 1)), reads=[bKT, b_F2], writes=[bb])
                        op("scalar", lambda e, bk=bk, mc=mc, pt=pt, Lg=Lg: e.activation(
                            out=pt[:, mc, 0:Lg], in_=bk[:, 0:Lg], func=AF.Exp, scale=1.0 / 16.0), reads=[bb], writes=[bpt])
                    obk = []
                    for dc in range(2):
                        bk, bb = next_bank()
                        obk.append((bk, bb))
                        for mc in range(2):
                            op("tensor", lambda e, bk=bk, mc=mc, dc=dc, h=h, ks=ks, pt=pt, Lg=Lg: e.matmul(
                                bk[:, 0:Lg], lhsT=V[ks][:, mc, h * 256 + dc * 128:h * 256 + (dc + 1) * 128], rhs=pt[:, mc, 0:Lg],
                                start=(mc == 0), stop=(mc == 1)), reads=[bV, bpt], writes=[bb])
                    bkd, bbd = next_bank()
                    for mc in range(2):
                        op("tensor", lambda e, bkd=bkd, mc=mc, pt=pt, Lg=Lg: e.matmul(
                            bkd[:, 0:Lg], lhsT=ones_bf[:, :], rhs=pt[:, mc, 0:Lg], start=(mc == 0), stop=(mc == 1)),
                           reads=[buf("ones"), bpt], writes=[bbd])
                    op("scalar", lambda e, bkd=bkd, Lg=Lg: e.activation(out=RD[:, 0:Lg], in_=bkd[:, 0:Lg], func=AF.Ln),
                       reads=[bbd], writes=[bRD])
                    op("scalar", lambda e, Lg=Lg: e.activation(out=RD[:, 0:Lg], in_=RD[:, 0:Lg], func=AF.Exp, scale=-1.0),
                       reads=[bRD], writes=[bRD])
                    for dc in range(2):
                        bk, bb = obk[dc]
                        op("vector", lambda e, bk=bk, dc=dc, h=h, t0=t0, t1=t1, Lg=Lg: e.tensor_tensor(
                            out=F1[:, h * 2 + dc, c0 + t0:c0 + t1], in0=bk[:, 0:Lg], in1=RD[:, 0:Lg], op=ALU.mult),
                           reads=[bb, bRD], writes=[b_F1])
            yield from proj_tok_resid_ln(xb_i, subs, SP, c0, F1, b_F1, ("mo0", "mo1"), 1, hx)
            yield None
            transposes_to_feat(xb_i, subs, SP, c0, F0, b_F0)
            if flusher:
                start_ln3_steps(prefetch)
            for j6 in range(6):
                rgg, brgg = yield "g%d" % j6
                rgu, brgu = yield "u%d" % j6
                nm = 4 if j6 < 5 else 2
                rl = 512 if j6 < 5 else 256
                for m in range(nm):
                    bkg, bbg = next_bank()
                    for kc in range(8):
                        op("tensor", lambda e, bkg=bkg, m=m, kc=kc, rgg=rgg, rl=rl: e.matmul(
                            bkg[:, 0:Tn], lhsT=rgg[:, kc * rl + m * 128:kc * rl + (m + 1) * 128], rhs=F0[:, kc, cs],
                            start=(kc == 0), stop=(kc == 7)), reads=[b_F0, brgg], writes=[bbg])
                    bku, bbu2 = next_bank()
                    for kc in range(8):
                        op("tensor", lambda e, bku=bku, m=m, kc=kc, rgu=rgu, rl=rl: e.matmul(
                            bku[:, 0:Tn], lhsT=rgu[:, kc * rl + m * 128:kc * rl + (m + 1) * 128], rhs=F0[:, kc, cs],
                            start=(kc == 0), stop=(kc == 7)), reads=[b_F0, brgu], writes=[bbu2])
                    f = j6 * 4 + m
                    sgs = sg[f % 2][:, h0:h0 + Tn]
                    bsg = buf("sg%d%s" % (f % 2, sfx))
                    op("scalar", lambda e, bkg=bkg, sgs=sgs: e.activation(out=sgs, in_=bkg[:, 0:Tn], func=AF.Silu),
                       reads=[bbg], writes=[bsg])
                    op("vector", lambda e, bku=bku, sgs=sgs, f=f: e.tensor_tensor(
                        out=hT[:, f, cs], in0=bku[:, 0:Tn], in1=sgs, op=ALU.mult),
                       reads=[bbu2, bsg], writes=[b_hT])
                    if flusher:
                        ln3_step()
            if flusher:
                while ln3_steps:
                    ln3_step()
            dbk = [[(banks[hx * 4 + j * 2 + half], b_bank[hx * 4 + j * 2 + half]) for half in range(2)] for j in range(NS)]
            for j6 in range(6):
                rg, brg = yield "d%d" % j6
                nk = 4 if j6 < 5 else 2
                for kl in range(nk):
                    k = j6 * 4 + kl
                    for j in range(NS):
                        for half in range(2):
                            bk, bb = dbk[j][half]
                            op("tensor", lambda e, bk=bk, j=j, half=half, k=k, kl=kl, rg=rg: e.matmul(
                                bk[0:SP, :], lhsT=hT[:, k, c0 + j * SP:c0 + (j + 1) * SP],
                                rhs=rg[:, kl * 1024 + half * 512:kl * 1024 + (half + 1) * 512],
                                start=(k == 0), stop=(k == NKFF - 1)), reads=[b_hT, brg], writes=[bb])
            for j, s in enumerate(subs):
                for half in range(2):
                    bk, bb = dbk[j][half]
                    op("vector", lambda e, bk=bk, s=s, half=half: e.scalar_tensor_tensor(
                        out=xbuf[xb_i][0:SP, s, half * 512:(half + 1) * 512],
                        in0=xbuf[xb_i][0:SP, s, half * 512:(half + 1) * 512], scalar=ALPHA, in1=bk[0:SP, :],
                        op0=ALU.mult, op1=ALU.add), reads=[bb, b_xb[xb_i][s]], writes=[b_xb[xb_i][s]])
            for j, s in enumerate(subs):
                pending.append((xb_i, s, SP, y_dst[j * SP:(j + 1) * SP, :]))

        pending = []
        ln3_steps = []
        st63 = T("st63", [128, 4, 12], F32)
        mv3 = T("mv3", [128, 4, 2], F32)
        ve3 = T("ve3", [128, 4, 2], F32)
        rs3 = T("rs3", [128, 4], F32)

        def ln3_stage(stg, q, xb_i, s, SP, dst):
            xv = xbuf[xb_i]
            bx = b_xb[xb_i][s]
            bS, bM, bV, bR = buf("st63_%d" % q), buf("mv3_%d" % q), buf("ve3_%d" % q), buf("rs3_%d" % q)
            if stg == 0:
                for hh in range(2):
                    op("vector", lambda e, hh=hh: e.bn_stats(out=st63[0:SP, q, hh * 6:(hh + 1) * 6], in_=xv[0:SP, s, hh * 512:(hh + 1) * 512]),
                       reads=[bx], writes=[bS])
                op("vector", lambda e: e.bn_aggr(out=mv3[0:SP, q, :], in_=st63[0:SP, q, :]), reads=[bS], writes=[bM])
            elif stg == 1:
                op("scalar", lambda e: e.activation(out=ve3[0:SP, q, 0:1], in_=mv3[0:SP, q, 1:2], func=AF.Ln, bias=epsb[0:SP, 0:1], scale=1.0),
                   reads=[bM, buf("epsb")], writes=[bV])
                op("scalar", lambda e: e.activation(out=rs3[0:SP, q:q + 1], in_=ve3[0:SP, q, 0:1], func=AF.Exp, scale=-0.5),
                   reads=[bV], writes=[bR])
            elif stg == 2:
                op("vector", lambda e: e.scalar_tensor_tensor(out=xv[0:SP, s, :], in0=xv[0:SP, s, :], scalar=mv3[0:SP, q, 0:1],
                                                              in1=lng[2][0:SP, :], op0=ALU.subtract, op1=ALU.mult),
                   reads=[bx, bM, buf("lng2")], writes=[bx])
            elif stg == 3:
                op("vector", lambda e: e.scalar_tensor_tensor(out=xv[0:SP, s, :], in0=xv[0:SP, s, :], scalar=rs3[0:SP, q:q + 1],
                                                              in1=lnb[2][0:SP, :], op0=ALU.mult, op1=ALU.add),
                   reads=[bx, bR, buf("lnb2")], writes=[bx])
            else:
                op("sync", lambda e: e.dma_start(out=dst, in_=xv[0:SP, s, :]), reads=[bx], dma="yst%d" % xb_i)

        def start_ln3_steps(prefetch):
            items = list(pending)
            del pending[:]
            n = len(items)
            for k in range(n + 4 if n else 0):
                def step(k=k):
                    for q, it in enumerate(items):
                        if 0 <= k - q < 5:
                            ln3_stage(k - q, q, *it)
                ln3_steps.append(step)
            if prefetch is not None:
                ln3_steps.append(prefetch)

        def ln3_step():
            if ln3_steps:
                ln3_steps.pop(0)()

        def flush_pending():
            start_ln3_steps(None)
            while ln3_steps:
                ln3_step()

        def co_run(gens):
            granted = {}
            pos = [ring_state["used"]] * len(gens)
            req = [next(g) for g in gens]
            alive = [True] * len(gens)
            while any(alive):
                for gi, g in enumerate(gens):
                    if not alive[gi]:
                        continue
                    try:
                        if req[gi] is None:
                            req[gi] = g.send(None)
                            if req[gi] is None:
                                continue
                        name = req[gi]
                        i = pos[gi]
                        if i not in granted:
                            assert i == ring_state["used"], (i, ring_state)
                            granted[i] = use_piece(name)
                        assert seq[i] == name, (seq[i], name)
                        pos[gi] += 1
                        req[gi] = g.send(granted[i])
                    except StopIteration:
                        alive[gi] = False

        def k_transposes(src_sub0, dstT, b_dst, col0):
            for mc in range(2):
                for g in range(2):
                    bk, bb = next_bank()
                    for c4 in range(4):
                        c = g * 4 + c4
                        op("tensor", lambda e, bk=bk, mc=mc, c=c, c4=c4: e.transpose(
                            bk[:, c4 * 128:(c4 + 1) * 128], xbuf[1][:, src_sub0 + mc, c * 128:(c + 1) * 128], ident[:]),
                           reads=[b_xb[1][src_sub0 + mc], buf("ident")], writes=[bb])
                    op("scalar", lambda e, bk=bk, mc=mc, g=g: e.activation(
                        out=dstT[:, g * 4:(g + 1) * 4, col0 + mc * 128:col0 + (mc + 1) * 128],
                        in_=bk[:].rearrange("p (c t) -> p c t", c=4), func=AF.Copy), reads=[bb], writes=[b_dst])

        def load_x(t):
            xb_i = t % 2
            if t < NT:
                op("sync", lambda e: e.dma_start(out=xbuf[xb_i][:], in_=xp[t * 512:(t + 1) * 512, :].rearrange("(s p) d -> p s d", p=128)),
                   writes=b_xb[xb_i], dma="x%d" % xb_i)
            else:
                op("sync", lambda e: e.dma_start(out=xbuf[xb_i][0:64, 0, :], in_=xs), writes=b_xb[xb_i], dma="x%d" % xb_i)

        load_x(0)
        if NT > 0:
            op("gpsimd", lambda e: e.dma_start(out=xbuf[1][:, 0:2, :], in_=memp.rearrange("(c p) d -> p c d", p=128)),
               writes=[b_xb[1][0], b_xb[1][1]], dma="mld")
        for b in range(2):
            op("gpsimd", lambda e, b=b: e.dma_start(out=xbuf[1][:, 2:4, :], in_=ck[b].rearrange("(c p) d -> p c d", p=128)),
               writes=[b_xb[1][2], b_xb[1][3]], dma="kld")
            k_transposes(2, KT[b], buf("KT%d" % b), 0)
        emit_ring_loads(LOOKAHEAD)
        for b in range(2):
            op("gpsimd", lambda e, b=b: e.dma_start(out=V[b][:], in_=cv[b].rearrange("(c p) d -> p c d", p=128)),
               writes=[buf("V%d" % b)], dma="vld%d" % b)

        if NT > 0:
            b_F2 = buf("F2_0")
            k_transposes(0, F2, b_F2, 0)
            for wi, (pn, dsto) in enumerate((("mk", mk_o), ("mv", mv_o))):
                for half in range(2):
                    rg, brg = use_piece("%s%d" % (pn, half))
                    if wi == 0:
                        for m in range(4):
                            bk, bb = next_bank()
                            for kc in range(8):
                                op("tensor", lambda e, bk=bk, m=m, kc=kc, rg=rg: e.matmul(
                                    bk[:, 0:256], lhsT=rg[:, kc * 512 + m * 128:kc * 512 + (m + 1) * 128], rhs=F2[:, kc, 0:256],
                                    start=(kc == 0), stop=(kc == 7)), reads=[b_F2, brg], writes=[bb])
                            op("scalar", lambda e, bk=bk, half=half, m=m: e.activation(
                                out=KT[2][:, half * 4 + m, :], in_=bk[:, 0:256], func=AF.Copy), reads=[bb], writes=[buf("KT2")])
                    for mc in range(2):
                        bk, bb = next_bank()
                        for kc in range(8):
                            op("tensor", lambda e, bk=bk, mc=mc, kc=kc, rg=rg: e.matmul(
                                bk[:, :], lhsT=F2[:, kc, mc * 128:(mc + 1) * 128], rhs=rg[:, kc * 512:(kc + 1) * 512],
                                start=(kc == 0), stop=(kc == 7)), reads=[b_F2, brg], writes=[bb])
                        sl = mc % 2
                        op("scalar", lambda e, bk=bk, sl=sl: e.activation(out=sg[sl][:], in_=bk[:], func=AF.Copy),
                           reads=[bb], writes=both("sg%d" % sl))
                        op("sync", lambda e, sl=sl, mc=mc, half=half, dsto=dsto: e.dma_start(
                            out=dsto[mc * 128:(mc + 1) * 128, half * 512:(half + 1) * 512], in_=sg[sl][:]),
                           reads=both("sg%d" % sl), dma="o_kv%d" % sl)
                        if wi == 1:
                            op("vector", lambda e, bk=bk, mc=mc, half=half: e.tensor_copy(
                                out=V[2][:, mc, half * 512:(half + 1) * 512], in_=bk[:]), reads=[bb], writes=[buf("V2")])
        else:
            for n in kv_seq:
                use_piece(n)

        for t in range(NT):
            xb_i = t % 2
            pf = (lambda t=t: load_x(t + 1))
            gA = run_tile(xb_i, [0, 1], 128, 0, [(0, 256)], [2], False, yp[t * 512:t * 512 + 256, :], False, None)
            gB = run_tile(xb_i, [2, 3], 128, 1, [(0, 256)], [2], False, yp[t * 512 + 256:(t + 1) * 512, :], t == NT - 1, pf, True)
            co_run([gA, gB])
        if NT == 0:
            pass
        gS = run_tile(NT % 2, [0], 64, 0, [(0, 32), (32, 64)], [0, 1], True, ys, False, None, True)
        co_run([gS])

        flush_pending()
        assert ring_state["used"] == len(seq), (ring_state, len(seq))
        ensure_converted(len(seq))
        S.emit()
    return nc


_CACHE = {}


def _get_nc(NT):
    if NT not in _CACHE:
        _CACHE[NT] = build(NT)
    return _CACHE[NT]


def make_in_maps(inputs, NT):
    f = lambda a: np.ascontiguousarray(np.asarray(a, dtype=np.float32))
    shared = {
        "w_in": f(inputs["w_in"][0]),
        "sgu_ln_g": f(inputs["sgu_ln_g"][0]).reshape(1, 512),
        "sgu_ln_b": f(inputs["sgu_ln_b"][0]).reshape(1, 512),
        "w_s": f(inputs["w_s"][0]),
        "b_s": f(inputs["b_s"][0]).reshape(1, 512),
        "w_conv": f(inputs["w_conv"][0]),
        "w_out": f(inputs["w_out"][0]),
        "w_q": f(inputs["w_q"][0]),
        "w_mem_k": f(inputs["w_mem_k"][0]),
        "w_mem_v": f(inputs["w_mem_v"][0]),
        "w_mem_o": f(inputs["w_mem_o"][0]),
        "w_gate": f(inputs["w_gate"][0]),
        "w_up": f(inputs["w_up"][0]),
        "w_down": f(inputs["w_down"][0]),
    }
    for n in ("ln_mix_g", "ln_mix_b", "ln_mem_g", "ln_mem_b", "ln_ffn_g", "ln_ffn_b"):
        shared[n] = f(inputs[n][0]).reshape(1, D)
    in_maps = []
    for c in range(NCORES):
        m = dict(shared)
        m["x_prompt"] = f(inputs["x_prompt"][c, :NT * 512])
        m["x_sample"] = f(inputs["x_sample"][2 * c:2 * c + 2]).reshape(64, D)
        m["cache_k"] = f(inputs["cache_mem_k"][0, 2 * c:2 * c + 2]).reshape(2, 256, D)
        m["cache_v"] = f(inputs["cache_mem_v"][0, 2 * c:2 * c + 2]).reshape(2, 256, D)
        m["state_conv"] = f(inputs["state_conv"][0, 2 * c:2 * c + 2]).reshape(4, 512)
        m["mem_prompt"] = f(inputs["mem_prompt"][c])
        in_maps.append(m)
    return in_maps


def run(inputs, NT):
    nc = _get_nc(NT)
    in_maps = make_in_maps(inputs, NT)
    res = run_bass_kernel_spmd(nc, in_maps, core_ids=list(range(NCORES)))
    R = res.results
    y_prompt = np.stack([R[c]["y_prompt"] for c in range(NCORES)]).reshape(NCORES, NT * 512, D)
    y_sample = np.concatenate([R[c]["y_sample"].reshape(2, 32, D) for c in range(NCORES)], axis=0)
    mk = np.stack([R[c]["mem_k_prompt"].reshape(256, 4, 256) for c in range(NCORES)])[None]
    mv = np.stack([R[c]["mem_v_prompt"].reshape(256, 4, 256) for c in range(NCORES)])[None]
    cvp = np.stack([R[c]["conv_prompt"].reshape(2, 512) for c in range(NCORES)])[None]
    cvs = np.concatenate([R[c]["conv_sample"].reshape(2, 2, 512) for c in range(NCORES)], axis=0)[None]
    sv = np.concatenate([R[c]["sgu_v_sample"].reshape(2, 32, 4, 128) for c in range(NCORES)], axis=0)[None]
    outs = (y_prompt, y_sample, mk, mv, cvp, cvs, sv)
    return tuple(np.ascontiguousarray(o.astype(np.float32)) for o in outs)


def kernel(**inputs):
    return run(inputs, 16)
```
